# Optimizing a Trainium2 kernel written in Bass

```python
import math
import jax, jax.numpy as jnp
from jax import lax
import numpy as np

D_MODEL = 1024
BATCH = 8
SEQ = 2048
DEPTH = 1
DEC_BATCH = 128
DEC_SEQ = 8
PAST_LEN = 16384
PAGE_SIZE = 128

HEAD_DIM = 64
D_A = D_MODEL // 2
H_A = D_A // HEAD_DIM
D_B = D_MODEL // 2
H_B = D_B // HEAD_DIM
LORA_W = 64
LORA_A = 64
CONV_W = 4
MLSTM_CHUNK = 64
NORM_EPS = 1e-6
RWKV_GN_EPS = 64e-5
MLSTM_GN_EPS = 1e-6

SHIFT_W = 3 * D_A + LORA_W + LORA_A
CONV_COLS = 2 * D_B
IN_SIZES = (SHIFT_W, D_A, CONV_COLS, D_B, D_B, H_B, H_B, D_B, D_MODEL, D_MODEL)
N_IN = sum(IN_SIZES)
SHIFT_SIZES = (D_A, D_A, D_A, LORA_W, LORA_A)

kernel_name = "rwkv7_mlstm_gated_parallel_step"


def split_cols(p, sizes):
    idx = np.cumsum(sizes)[:-1].tolist()
    return jnp.split(p, idx, axis=-1)


def rmsnorm(x, g):
    xf = x.astype(jnp.float32)
    y = xf * lax.rsqrt(jnp.mean(xf * xf, axis=-1, keepdims=True) + NORM_EPS)
    return (y * g.astype(jnp.float32)).astype(x.dtype)


def head_layernorm(y, w, b, eps):
    mu = jnp.mean(y, axis=-1, keepdims=True)
    var = jnp.mean(jnp.square(y - mu), axis=-1, keepdims=True)
    H, d = y.shape[-2:]
    return (y - mu) * lax.rsqrt(var + eps) * w.reshape(H, d) + b.reshape(H, d)


def head_rmsnorm(y, w, eps):
    H, d = y.shape[-2:]
    return y * lax.rsqrt(jnp.mean(y * y, axis=-1, keepdims=True) + eps) * w.reshape(H, d)


def rwkv7_mix(r, k, v, wl, al, S0, w_decay2, w0, w_iclr2, a0, k_k, k_a, r_k, ln_w, ln_b):
    f32 = jnp.float32
    B, T, _ = r.shape
    r, k, v, wl, al = (t.astype(f32) for t in (r, k, v, wl, al))
    w = w0.astype(f32) + jnp.tanh(wl) @ w_decay2.astype(f32)
    decay = jnp.exp(-jnp.exp(-jax.nn.softplus(-w) - 0.5))
    a = jax.nn.sigmoid(a0.astype(f32) + al @ w_iclr2.astype(f32))
    heads = lambda t: t.reshape(B, T, H_A, HEAD_DIM)
    kk = heads(k * k_k.astype(f32))
    kk = kk / jnp.maximum(jnp.sqrt(jnp.sum(kk * kk, axis=-1, keepdims=True)), 1e-12)
    k = k * (1.0 + (a - 1.0) * k_a.astype(f32))
    r_h, d_h, k_h, v_h, a_h = heads(r), heads(decay), heads(k), heads(v), heads(a)

    def step(S, inp):
        rt, dt, kt, vt, kkt, at = inp
        sk = jnp.einsum('bhij,bhj->bhi', S, kkt)
        S = (S * dt[:, :, None, :]
             - sk[..., :, None] * (kkt * at)[..., None, :]
             + vt[..., :, None] * kt[..., None, :])
        return S, jnp.einsum('bhij,bhj->bhi', S, rt)

    xs = tuple(jnp.moveaxis(t, 1, 0) for t in (r_h, d_h, k_h, v_h, kk, a_h))
    S_T, y = lax.scan(step, S0.astype(f32), xs)
    y = jnp.moveaxis(y, 0, 1)
    y = head_layernorm(y, ln_w.astype(f32), ln_b.astype(f32), RWKV_GN_EPS)
    bonus = jnp.sum(r_h * k_h * r_k.astype(f32).reshape(H_A, HEAD_DIM), axis=-1, keepdims=True)
    y = y + bonus * v_h
    return y.reshape(B, T, D_A), S_T


def mlstm_chunkwise(q, k, v, i_pre, logf, C0, n0, m0):
    f32 = jnp.float32
    B, T, H, d = q.shape
    L = math.gcd(T, MLSTM_CHUNK)
    NC = T // L
    q, v = q.astype(f32), v.astype(f32)
    k = k.astype(f32) * (1.0 / math.sqrt(d))
    ch4 = lambda t: t.reshape(B, NC, L, H, d).transpose(1, 0, 3, 2, 4)
    ch3 = lambda t: t.astype(f32).reshape(B, NC, L, H).transpose(1, 0, 3, 2)
    causal = jnp.tril(jnp.ones((L, L), dtype=bool))

    def step(carry, inp):
        C, n, m = carry
        qc, kc, vc, ic, fc = inp
        b = jnp.cumsum(fc, axis=-1)
        D = b[..., :, None] - b[..., None, :] + ic[..., None, :]
        D = jnp.where(causal, D, -jnp.inf)
        inter = b + m[..., None]
        m_t = jnp.maximum(jnp.max(D, axis=-1), inter)
        w_ts = jnp.exp(D - m_t[..., None])
        w_in = jnp.exp(inter - m_t)
        s = jnp.einsum('bhtd,bhsd->bhts', qc, kc) * w_ts
        num = (jnp.einsum('bhts,bhsd->bhtd', s, vc)
               + w_in[..., None] * jnp.einsum('bhij,bhtj->bhti', C, qc))
        den = jnp.sum(s, axis=-1) + w_in * jnp.einsum('bhj,bhtj->bht', n, qc)
        h = num / jnp.maximum(jnp.abs(den), jnp.exp(-m_t))[..., None]
        bL = b[..., -1]
        g = bL[..., None] - b + ic
        m_new = jnp.maximum(bL + m, jnp.max(g, axis=-1))
        ws = jnp.exp(g - m_new[..., None])
        dec = jnp.exp(bL + m - m_new)
        C_new = dec[..., None, None] * C + jnp.einsum('bhs,bhsi,bhsj->bhij', ws, vc, kc)
        n_new = dec[..., None] * n + jnp.einsum('bhs,bhsj->bhj', ws, kc)
        return (C_new, n_new, m_new), h

    carry0 = (C0.astype(f32), n0.astype(f32), m0.astype(f32))
    (C_T, n_T, m_T), h = lax.scan(step, carry0, (ch4(q), ch4(k), ch4(v), ch3(i_pre), ch3(logf)))
    h = h.transpose(1, 0, 3, 2, 4).reshape(B, T, H, d)
    return h, C_T, n_T, m_T


def mixer_layer(x, c, shift0, S0, conv0, C0, n0, m0,
                g_norm, w_ada, b_ada, w_in, mu_shift, w_decay2, w0, w_iclr2, a0, k_k, k_a, r_k,
                ln_w, ln_b, conv_w, conv_b, b_i, b_f, gn_w, w_up_a, w_up_b, w_out):
    B, T, _ = x.shape
    f32 = jnp.float32
    ada_shift, ada_scale, ada_gate = jnp.split(jax.nn.silu(c) @ w_ada + b_ada, 3, axis=-1)
    h = rmsnorm(x, g_norm) * (1.0 + ada_scale[:, None]) + ada_shift[:, None]
    p = h @ w_in
    p_shift, z_a, qk_pre, v_b, o_b, i_b, f_b, z_b, gl_a, gl_b = split_cols(p, IN_SIZES)

    prev = jnp.concatenate([shift0[:, None].astype(p.dtype), p_shift[:, :-1]], axis=1)
    ps = p_shift + mu_shift * (prev - p_shift)
    r, k, v, wl, al = split_cols(ps, SHIFT_SIZES)
    y_a, S_T = rwkv7_mix(r, k, v, wl, al, S0, w_decay2, w0, w_iclr2, a0, k_k, k_a, r_k, ln_w, ln_b)
    out_a = (y_a * jax.nn.silu(z_a.astype(f32))).astype(x.dtype)

    buf = jnp.concatenate([conv0.astype(qk_pre.dtype), qk_pre], axis=1)
    conv = conv_b + sum(buf[:, j:j + T] * conv_w[j] for j in range(CONV_W))
    qk = jax.nn.silu(conv)
    q_b, k_b = jnp.split(qk, 2, axis=-1)
    hds = lambda t: t.reshape(B, T, H_B, HEAD_DIM)
    logf = jax.nn.log_sigmoid((f_b + b_f).astype(f32))
    i_pre = (i_b + b_i).astype(f32)
    hb, C_T, n_T, m_T = mlstm_chunkwise(hds(q_b), hds(k_b), hds(v_b), i_pre, logf, C0, n0, m0)
    hb = head_rmsnorm(hb, gn_w.astype(f32), MLSTM_GN_EPS).reshape(B, T, D_B)
    out_b = (jax.nn.sigmoid(o_b.astype(f32)) * hb * jax.nn.silu(z_b.astype(f32))).astype(x.dtype)

    merged = jax.nn.sigmoid(gl_a) * (out_a @ w_up_a) + jax.nn.sigmoid(gl_b) * (out_b @ w_up_b)
    x = x + ada_gate[:, None] * (merged @ w_out)
    conv_T = buf[:, buf.shape[1] - (CONV_W - 1):]
    return x, (p_shift[:, -1], S_T, conv_T, C_T, n_T, m_T)


def trunk(x, c, states, params, g_final):
    new = [[] for _ in range(len(states))]
    for l in range(DEPTH):
        st = tuple(s[l] for s in states)
        pl = tuple(w[l] for w in params)
        x, st_new = mixer_layer(x, c, *st, *pl)
        for lst, s in zip(new, st_new):
            lst.append(s.astype(x.dtype))
    return rmsnorm(x, g_final), tuple(jnp.stack(lst) for lst in new)


def setup_inputs(seed: int = 0) -> dict:
    key = jax.random.key(seed)
    ks = iter(jax.random.split(key, 48))
    f32 = jnp.float32
    nrm = lambda shape, s: jax.random.normal(next(ks), shape, f32) * s
    L = DEPTH
    return {
        "x_prompt": nrm((BATCH, SEQ, D_MODEL), 1.0),
        "x_sample": nrm((DEC_BATCH, DEC_SEQ, D_MODEL), 1.0),
        "c_prompt": nrm((BATCH, D_MODEL), 1.0),
        "c_sample": nrm((DEC_BATCH, D_MODEL), 1.0),
        "state_rwkv_shift": nrm((L, DEC_BATCH, SHIFT_W), 1.0),
        "state_rwkv_S": nrm((L, DEC_BATCH, H_A, HEAD_DIM, HEAD_DIM), 0.1),
        "state_mlstm_conv": nrm((L, DEC_BATCH, CONV_W - 1, CONV_COLS), 1.0),
        "state_mlstm_C": nrm((L, DEC_BATCH, H_B, HEAD_DIM, HEAD_DIM), 0.1),
        "state_mlstm_n": nrm((L, DEC_BATCH, H_B, HEAD_DIM), 0.1),
        "state_mlstm_m": nrm((L, DEC_BATCH, H_B), 1.0),
        "g_norm": 1.0 + nrm((L, D_MODEL), 0.01),
        "w_ada": nrm((L, D_MODEL, 3 * D_MODEL), 0.5 * D_MODEL ** -0.5),
        "b_ada": nrm((L, 3 * D_MODEL), 0.02),
        "w_in": nrm((L, D_MODEL, N_IN), D_MODEL ** -0.5),
        "mu_shift": jax.random.uniform(next(ks), (L, SHIFT_W), f32),
        "w_decay2": nrm((L, LORA_W, D_A), 0.1),
        "w0": jnp.linspace(-2.5, 1.5, D_A, dtype=f32)[None] + nrm((L, D_A), 0.1),
        "w_iclr2": nrm((L, LORA_A, D_A), 0.1),
        "a0": nrm((L, D_A), 0.1),
        "k_k": 0.85 + nrm((L, D_A), 0.02),
        "k_a": 1.0 + nrm((L, D_A), 0.02),
        "r_k": nrm((L, D_A), 0.1),
        "ln_w": 1.0 + nrm((L, D_A), 0.02),
        "ln_b": nrm((L, D_A), 0.02),
        "conv_w": nrm((L, CONV_W, CONV_COLS), CONV_W ** -0.5),
        "conv_b": nrm((L, CONV_COLS), 0.02),
        "b_i": nrm((L, H_B), 0.1),
        "b_f": jnp.linspace(3.0, 6.0, H_B, dtype=f32)[None] + nrm((L, H_B), 0.1),
        "gn_w": 1.0 + nrm((L, D_B), 0.02),
        "w_up_a": nrm((L, D_A, D_MODEL), D_A ** -0.5),
        "w_up_b": nrm((L, D_B, D_MODEL), D_B ** -0.5),
        "w_out": nrm((L, D_MODEL, D_MODEL), D_MODEL ** -0.5),
        "g_final": 1.0 + nrm((D_MODEL,), 0.01),
    }


def reference(x_prompt, x_sample, c_prompt, c_sample,
              state_rwkv_shift, state_rwkv_S, state_mlstm_conv, state_mlstm_C, state_mlstm_n, state_mlstm_m,
              g_norm, w_ada, b_ada, w_in, mu_shift, w_decay2, w0, w_iclr2, a0, k_k, k_a, r_k,
              ln_w, ln_b, conv_w, conv_b, b_i, b_f, gn_w, w_up_a, w_up_b, w_out, g_final):
    params = (g_norm, w_ada, b_ada, w_in, mu_shift, w_decay2, w0, w_iclr2, a0, k_k, k_a, r_k,
              ln_w, ln_b, conv_w, conv_b, b_i, b_f, gn_w, w_up_a, w_up_b, w_out)
    Bp = x_prompt.shape[0]
    dt = x_prompt.dtype
    prompt_states = (
        jnp.zeros((DEPTH, Bp, SHIFT_W), dt),
        jnp.zeros((DEPTH, Bp, H_A, HEAD_DIM, HEAD_DIM), dt),
        jnp.zeros((DEPTH, Bp, CONV_W - 1, CONV_COLS), dt),
        jnp.zeros((DEPTH, Bp, H_B, HEAD_DIM, HEAD_DIM), dt),
        jnp.zeros((DEPTH, Bp, H_B, HEAD_DIM), dt),
        jnp.zeros((DEPTH, Bp, H_B), dt),
    )
    sample_states = (state_rwkv_shift, state_rwkv_S, state_mlstm_conv,
                     state_mlstm_C, state_mlstm_n, state_mlstm_m)
    y_prompt, (p_shift, p_S, p_conv, p_C, p_n, p_m) = trunk(x_prompt, c_prompt, prompt_states, params, g_final)
    y_sample, (s_shift, s_S, s_conv, s_C, s_n, s_m) = trunk(x_sample, c_sample, sample_states, params, g_final)
    return (y_prompt, y_sample, p_shift, p_S, p_conv, p_C, p_n, p_m, s_shift, s_S, s_conv, s_C, s_n, s_m)
```

```python
import numpy as np
import concourse.bass as bass
import concourse.mybir as mybir
from concourse.bass_utils import run_bass_kernel_spmd

F32 = mybir.dt.float32
BF16 = mybir.dt.bfloat16
ALU = mybir.AluOpType
AF = mybir.ActivationFunctionType
AX = mybir.AxisListType

NCORES = 8
D = 1024
NT = 17
SHIFT_W = 1664
NRES = 4752
C_ZA, C_QK, C_VB, C_OB, C_I, C_F, C_ZB, C_GLA, C_GLB = 1664, 2176, 3200, 3712, 4224, 4232, 4240, 4752, 5776
EXPM05 = float(np.exp(-0.5))
NEG = -30000.0
RW_W = 2
MX = BF16


class Src:
    def __init__(self, sem, scale):
        self.sem, self.scale, self.count = sem, scale, 0


class Eng:
    def __init__(self, name, h, src, inorder_safe=False):
        self.name, self.h, self.src, self.safe = name, h, src, inorder_safe
        self.clock = {}


class KB:
    def __init__(self, nc):
        self.nc = nc
        mk = lambda n: Src(nc.alloc_semaphore(n), 1)
        self.pe = Eng("pe", nc.tensor, mk("s_pe"), True)
        self.act = Eng("act", nc.scalar, mk("s_act"))
        self.dve = Eng("dve", nc.vector, mk("s_dve"))
        self.pool = Eng("pool", nc.gpsimd, mk("s_pool"))
        self.sp = Eng("sp", nc.sync, mk("s_sp"))
        self.engs = [self.pe, self.act, self.dve, self.pool, self.sp]
        self.dslots = [Src(nc.alloc_semaphore(f"s_dma{i}"), 16) for i in range(24)]
        self.dnext = 0
        self.swslots = []
        self.lastw = {}
        self.readers = {}

    def _need(self, eng, src, seq):
        if src is eng.src and eng.safe:
            return
        if eng.clock.get(id(src), 0) < seq:
            eng.h.wait_ge(src.sem, seq * src.scale)
            eng.clock[id(src)] = seq

    def _deps(self, eng, ins, outs):
        for r in ins:
            if r in self.lastw:
                self._need(eng, *self.lastw[r])
        for r in outs:
            if r in self.lastw:
                self._need(eng, *self.lastw[r])
            for (s, q) in self.readers.get(r, {}).values():
                self._need(eng, s, q)

    def _commit(self, src, ins, outs):
        src.count += 1
        seq = src.count
        for r in ins:
            self.readers.setdefault(r, {})[id(src)] = (src, seq)
        for r in outs:
            self.lastw[r] = (src, seq)
            self.readers[r] = {}
        return seq

    @staticmethod
    def _names(aps):
        return [a.name for a in aps if a is not None and hasattr(a, "name")]

    def op(self, eng, fn, outs, ins):
        o, i = self._names(outs), self._names(ins)
        self._deps(eng, i, o)
        fn().then_inc(eng.src.sem, 1)
        self._commit(eng.src, i, o)

    def dma(self, out, in_, eng=None, **kw):
        eng = eng or self.sp
        o, i = self._names([out]), self._names([in_])
        if eng is self.pool:
            slot = Src(self.nc.alloc_semaphore(f"s_sw{len(self.swslots)}"), 16)
            self.swslots.append(slot)
        else:
            slot = self.dslots[self.dnext]
            self.dnext = (self.dnext + 1) % len(self.dslots)
        self._deps(eng, i, o)
        if slot.count:
            self._need(eng, slot, slot.count)
        eng.h.dma_start(out=out, in_=in_, **kw).then_inc(slot.sem, 16)
        self._commit(slot, i, o)

    def finish(self):
        for slot in self.dslots + self.swslots:
            if slot.count:
                self._need(self.sp, slot, slot.count)
        for e in self.engs:
            for e2 in self.engs:
                if e2 is not e and e2.src.count:
                    self._need(e, e2.src, e2.src.count)

    def mm(self, out, lhsT, rhs, start=True, stop=True):
        self.op(self.pe, lambda: self.nc.tensor.matmul(out, lhsT=lhsT, rhs=rhs, start=start, stop=stop,
                                                       skip_group_check=True), [out], [lhsT, rhs])

    def actf(self, out, in_, func, bias=None, scale=1.0, accum=None):
        kw = {}
        if bias is not None:
            kw["bias"] = bias
        if accum is not None:
            kw["accum_out"] = accum
        ins = [in_] + ([bias] if hasattr(bias, "name") else []) + ([scale] if hasattr(scale, "name") else [])
        self.op(self.act, lambda: self.nc.scalar.activation(out=out, in_=in_, func=func, scale=scale, **kw),
                [out] + ([accum] if accum is not None else []), ins)

    def _v(self, e):
        return self.dve if e is None else e

    def tt(self, out, a, b, op, e=None):
        e = self._v(e)
        self.op(e, lambda: e.h.tensor_tensor(out=out, in0=a, in1=b, op=op), [out], [a, b])

    def ts(self, out, a, s1, s2=None, op0=ALU.mult, op1=None, e=None):
        e = self._v(e)
        ins = [a] + [s for s in (s1, s2) if hasattr(s, "name")]
        if op1 is None:
            self.op(e, lambda: e.h.tensor_scalar(out=out, in0=a, scalar1=s1, scalar2=None, op0=op0), [out], ins)
        else:
            self.op(e, lambda: e.h.tensor_scalar(out=out, in0=a, scalar1=s1, scalar2=s2, op0=op0, op1=op1),
                    [out], ins)

    def stt(self, out, a, s, b, op0, op1, e=None):
        e = self._v(e)
        ins = [a, b] + ([s] if hasattr(s, "name") else [])
        self.op(e, lambda: e.h.scalar_tensor_tensor(out=out, in0=a, scalar=s, in1=b, op0=op0, op1=op1), [out], ins)

    def cp(self, out, in_, e=None):
        e = self._v(e)
        self.op(e, lambda: e.h.tensor_copy(out=out, in_=in_), [out], [in_])

    def acp(self, out, in_):
        self.actf(out, in_, AF.Copy)

    def memset(self, out, val, e=None):
        e = self._v(e)
        self.op(e, lambda: e.h.memset(out, val), [out], [])

    def scan(self, out, d0, d1, init, op0, op1):
        ins = [d0, d1] + ([init] if hasattr(init, "name") else [])
        self.op(self.dve, lambda: self.nc.vector.tensor_tensor_scan(out=out, data0=d0, data1=d1, initial=init,
                                                                    op0=op0, op1=op1), [out], ins)

    def rsum(self, out, in_):
        self.op(self.dve, lambda: self.nc.vector.tensor_reduce(out=out, in_=in_, axis=AX.X, op=ALU.add), [out], [in_])

    def recip(self, out, in_):
        self.op(self.dve, lambda: self.nc.vector.reciprocal(out=out, in_=in_), [out], [in_])


def host_consts():
    c = {}
    t = np.arange(128)
    ident = np.eye(128, dtype=np.float32)
    c["ident"] = ident

    def masks(L):
        same = (t[:, None] // L) == (t[None, :] // L)
        strict = (same & (t[:, None] < t[None, :])).astype(np.float32)
        incl = (same & (t[:, None] <= t[None, :])).astype(np.float32)
        mA = np.concatenate([strict, incl], axis=1)
        mB = np.concatenate([strict, -incl], axis=1)
        mN = strict.T.copy()
        mbias = np.where(incl > 0, 0.0, NEG).astype(np.float32)
        nc_ = 128 // L
        rowm = (t[:, None] // L == np.arange(nc_)[None, :]).astype(np.float32)
        rst = np.broadcast_to((t % L != 0).astype(np.float32)[None], (128, 128)).copy()
        return mA, mB, mN, mbias, rowm, rst

    for nm, L in (("p", 64), ("s", 8)):
        mA, mB, mN, mbias, rowm, rst = masks(L)
        c["mA" + nm], c["mB" + nm], c["mN" + nm], c["rst" + nm] = mA, mB, mN, rst
        c["rowm" + nm] = rowm
    c["mbp"] = masks(128)[3]
    c["mbs"] = masks(8)[3]
    bo = np.zeros((128, 128), np.float32); bo[:64, :64] = 1; bo[64:, 64:] = 1
    c["blockones"] = bo
    bs = np.zeros((128, 2), np.float32); bs[:64, 0] = 1; bs[64:, 1] = 1
    c["blocksel"] = bs
    selh = np.zeros((8, 8, 128), np.float32)
    for h in range(8):
        selh[h, h, :] = 1
    c["selh"] = selh
    c["ones8"] = np.ones((8, 128), np.float32)
    c["i8"] = np.eye(8, dtype=np.float32)
    r8 = np.zeros((8, 128), np.float32); r8[:, t % 8 != 0] = 1
    c["rst8m"] = r8
    c["rst8a"] = np.where(r8 > 0, 0.0, -1e30).astype(np.float32)
    c["ones8t"] = np.ones((8, 128), np.float32)
    c["zeros8t"] = np.zeros((8, 128), np.float32)
    return c


_CONSTS = host_consts()


_LAST_SEQ = [None]


def build_two_pass(order=None, last_prompt=15):
    build_program(order, last_prompt, seq=None)
    return build_program(order, last_prompt, seq=_LAST_SEQ[0])


def build_program(order=None, last_prompt=15, seq=None):
    nc = bass.Bass("TRN2", target_bir_lowering=False)
    kb = KB(nc)
    di = lambda n, s: nc.dram_tensor(n, list(s), F32, kind="ExternalInput").ap()
    do = lambda n, s: nc.dram_tensor(n, list(s), F32, kind="ExternalOutput").ap()

    xall = di("xall", (NT, 128, D))
    cvec = di("cvec", (17, D))
    shift0 = di("shift0", (16, SHIFT_W))
    S0 = di("S0", (16, 8, 64, 64))
    conv0 = di("conv0", (48, 1024))
    C0 = di("C0", (16, 8, 64, 64))
    n0 = di("n0", (16, 8, 64))
    m0 = di("m0", (16, 8))
    w_ada = di("w_ada", (D, 3072))
    w_in = di("w_in", (D, 6800))
    epw = di("epw", (8, 128, 4096))
    wlora_d = di("wlora", (128, 512))
    pf = di("pfeat", (128, 64))
    prow = di("prow", (1, 6 * 1024))
    pg = di("pgate", (8, 4))
    cd = {k: di("c_" + k, v.shape) for k, v in _CONSTS.items()}

    y_o = do("y", (NT, 128, D))
    pshift_o = do("p_shift", (SHIFT_W,))
    pS_o = do("p_S", (8, 64, 64))
    pconv_o = do("p_conv", (3, 1024))
    pC_o = do("p_C", (8, 64, 64))
    pn_o = do("p_n", (8, 64))
    pm_o = do("p_m", (8, 1))
    sshift_o = do("s_shift", (16, SHIFT_W))
    sS_o = do("s_S", (16, 8, 64, 64))
    sconv_o = do("s_conv", (16, 3, 1024))
    sC_o = do("s_C", (16, 8, 64, 64))
    sn_o = do("s_n", (16, 8, 64))
    sm_o = do("s_m", (16, 8))
    epwb = nc.dram_tensor("epwb", [8, 128, 4096], BF16, kind="Internal").ap()

    sb = lambda n, s, dt=F32: nc.alloc_sbuf_tensor("sb_" + n, list(s), dt)
    pairs = [nc.alloc_psum_tensor(f"pp{i}", [128, 1024], F32) for i in range(4)]
    held = set()
    pi = [0]
    half = [None]

    def _next_pair():
        for _ in range(8):
            i = pi[0] % 4
            pi[0] += 1
            if i not in held:
                return i
        raise RuntimeError("no free PSUM pair")

    def pbank():
        if half[0] is None:
            i = _next_pair()
            half[0] = i
            return pairs[i][:, 0:512]
        i = half[0]
        half[0] = None
        return pairs[i][:, 512:1024]

    def ppair():
        half[0] = None
        return pairs[_next_pair()][:].rearrange("p (b c) -> p b c", b=2)

    def phold():
        half[0] = None
        i = _next_pair()
        held.add(i)
        return i, pairs[i][:].rearrange("p (b c) -> p b c", b=2)

    def prelease(i):
        held.discard(i)

    cs = {}
    for k, v in _CONSTS.items():
        cs[k] = sb("k_" + k, v.shape)
        kb.dma(cs[k][:], cd[k])
    ident = cs["ident"]
    pfeat = sb("pfeat", (128, 64))
    kb.dma(pfeat[:], pf)
    gT, bshT, bscT = pfeat[:, 0:8], pfeat[:, 8:16], pfeat[:, 16:24]
    muT, w0T, a0T, kkT, kaT, rkT, omkaT = (pfeat[:, 24:37], pfeat[:, 37:41], pfeat[:, 41:45], pfeat[:, 45:49],
                                           pfeat[:, 49:53], pfeat[:, 53:57], pfeat[:, 57:61])
    pconv = sb("pconv", (128, 40))
    kb.dma(pconv[:], di("pconv", (128, 40)))
    XY = sb("XY", (128, 2048))
    xo, yt = XY[:, 0:1024], XY[:, 1024:2048]
    rows = sb("rows", (128, 2560))
    kb.dma(rows[:], prow[:, 1024:3584].to_broadcast([128, 2560]))
    kb.dma(XY[:, 0:1024], prow[:, 0:1024].to_broadcast([128, 1024]))
    bgate_b, gfin_b = XY[:, 0:1024], rows[:, 0:1024]
    lnw_b, lnb_b, gnw_b = rows[:, 1024:1536], rows[:, 1536:2048], rows[:, 2048:2560]
    pgate = sb("pgate", (8, 4))
    kb.dma(pgate[:], pg)
    kb.ts(pgate[:, 1:2], pgate[:, 1:2], -1.0, None, ALU.mult)
    kb.ts(pfeat[:, 57:61], pfeat[:, 49:53], -1.0, 1.0, ALU.mult, ALU.add)
    wlora = sb("wlora", (128, 512))
    kb.dma(wlora[:], wlora_d)
    w_inb = nc.dram_tensor("w_inb", [D, NRES], BF16, kind="Internal").ap()
    for k in range(8):
        kb.dma(w_inb[k * 128:(k + 1) * 128, :], w_in[k * 128:(k + 1) * 128, 0:NRES], eng=kb.pool)
    w_inb_v = w_inb.rearrange("(k p) n -> p k n", p=128)
    stg = [sb(f"stg{i}", (128, 4096), BF16) for i in range(3)]
    sti = [0]

    TILE_PIECES = ([(c, 512) for c in (0, 512, 1024)] + [(1536, 128), (C_ZA, 512)] +
                   [(C_VB, 512), (C_OB, 512), (C_ZB, 512), (C_QK, 512), (C_QK + 512, 512), (C_I, 16)] +
                   [("e", m) for m in range(8)])
    ring = {"seq": list(seq) if seq is not None else [], "issued": 0, "used": 0, "rec": seq is None}

    def _pview(j):
        spec = ring["seq"][j]
        st = stg[j % 3]
        if spec[0] == "e":
            return st[:], epwb[spec[1]]
        col0, ncols = spec
        return (st[:, 0:8 * ncols].rearrange("p (k n) -> p k n", n=ncols), w_inb_v[:, :, col0:col0 + ncols])

    def next_piece(spec):
        j = ring["used"]
        if ring["rec"]:
            ring["seq"].append(spec)
        assert ring["seq"][j] == spec, (j, ring["seq"][j], spec)
        while ring["issued"] < min(len(ring["seq"]), j + 3):
            v, src = _pview(ring["issued"])
            kb.dma(v, src)
            ring["issued"] += 1
        ring["used"] += 1
        return _pview(j)[0]

    def wpiece(col0, ncols):
        return next_piece((col0, ncols))

    def epiece(m):
        return next_piece(("e", m))

    QK = sb("QK", (128, 8, 128), MX)
    TM = sb("TM", (128, 4, 512))
    PSb = sb("PSb", (128, 13, 128))
    PSs = sb("PSs", (128, 13, 128))
    hTf = sb("hTf", (128, 8, 128))
    crow = XY[0:17, 1024:2048]
    kb.dma(crow, cvec)
    csig = sb("csig", (17, D))
    kb.actf(csig[:], crow[:], AF.Sigmoid)
    kb.tt(csig[:], csig[:], crow[:], ALU.mult)
    sT = sb("sT", (128, 8, 17))
    pb = pbank()
    for k in range(8):
        kb.mm(pb[:, k * 17:(k + 1) * 17], csig[:, k * 128:(k + 1) * 128], ident[0:17, 0:17])
    kb.cp(sT[:].rearrange("p k s -> p (k s)"), pb[:, 0:136])
    sTp = TM[:, 0:2, :].rearrange("p a (k t) -> p (a k) t", t=128)
    sTs = PSb[:, 0:8, :]
    kb.cp(sTp, sT[:, :, 0:1].to_broadcast([128, 8, 128]))
    kb.cp(sTs.rearrange("p k (s t) -> p k s t", t=8), sT[:, :, 1:17].unsqueeze(3).to_broadcast([128, 8, 16, 8]))
    A_T = sb("A_T", (128, 8, 17))
    B_T = sb("B_T", (128, 8, 17))
    gate_p = sb("gate_p", (128, D))
    gate_s = sb("gate_s", (128, D))
    wst = [hTf, PSs[:, 0:8, :]]
    w_ada_v = w_ada.rearrange("(k p) n -> p k n", p=128)
    hA, ppA = phold()
    pbA, pbB = ppA[:, 0, :], ppA[:, 1, :]
    for blk in range(24):
        ws_ = wst[blk % 2]
        kb.dma(ws_[:] if blk % 2 == 0 else ws_, w_ada_v[:, :, blk * 128:(blk + 1) * 128])
        wv = (lambda k: ws_[:, k, :])
        if blk < 16:
            tg = pbA if blk < 8 else pbB
            fc = blk % 8
            for k in range(8):
                kb.mm(tg[:, fc * 17:(fc + 1) * 17], wv(k), sT[:, k, :], start=(k == 0), stop=(k == 7))
        else:
            n0_ = (blk - 16) * 128
            for (lt, dstg) in ((sTp, gate_p), (sTs, gate_s)):
                pb = pbank()
                for k in range(8):
                    kb.mm(pb[:, 0:128], lt[:, k, :], wv(k), start=(k == 0), stop=(k == 7))
                kb.tt(dstg[:, n0_:n0_ + 128], pb[:, 0:128], bgate_b[:, n0_:n0_ + 128], ALU.add)
    kb.cp(B_T[:].rearrange("p a s -> p (a s)"), pbA[:, 0:136])
    kb.cp(A_T[:].rearrange("p a s -> p (a s)"), pbB[:, 0:136])
    prelease(hA)
    kb.tt(B_T[:], B_T[:], bshT.unsqueeze(2).to_broadcast([128, 8, 17]), ALU.add)
    kb.tt(A_T[:], A_T[:], bscT.unsqueeze(2).to_broadcast([128, 8, 17]), ALU.add)
    kb.ts(A_T[:], A_T[:], 1.0, None, ALU.add)
    kb.tt(A_T[:], A_T[:], gT.unsqueeze(2).to_broadcast([128, 8, 17]), ALU.mult)
    dly = sb("dly", (8, 1))
    kb.cp(dly[:], A_T[0:8, 0, 0:1], e=kb.pool)
    for m in range(8):
        kb.dma(epwb[m], epw[m], eng=kb.pool)

    Hst = [sb(f"Hst{i}", (128, 64)) for i in range(4)]
    Cp = [sb(f"Cp{i}", (128, 65)) for i in range(4)]
    for i in range(4):
        kb.memset(Hst[i][:], 0.0)
        kb.memset(Cp[i][:], 0.0)
    carry_ps = sb("carry_ps", (128, 13))
    carry_qk = sb("carry_qk", (128, 8, 3))
    kb.memset(carry_ps[:], 0.0)
    kb.memset(carry_qk[:], 0.0)
    Bn_c = sb("Bn_c", (8, 1))
    M_c = sb("M_c", (8, 1))
    kb.memset(Bn_c[:], 0.0)
    kb.memset(M_c[:], 0.0)
    epsb = sb("epsb", (128, 4))
    kb.memset(epsb[:, 0:1], 1e-6)
    kb.memset(epsb[:, 1:2], 64e-5)
    kb.memset(epsb[:, 2:3], 1.0)
    kb.memset(epsb[:, 3:4], 1e-24)

    xt = [sb(f"xt{i}", (128, D)) for i in range(2)]
    ssqs = [sb(f"ssq{i}", (128, 4)) for i in range(2)]
    diags = [sb(f"diag{i}", (128, 128)) for i in range(2)]
    hTs = [sb(f"hT{i}", (128, 8, 128), BF16) for i in range(2)]
    QKb = sb("QKb", (128, 8, 176))
    OA = sb("OA", (128, 512))
    OB = sb("OB", (128, 512))
    OAb = sb("OAb", (128, 512), BF16)
    OBb = sb("OBb", (128, 512), BF16)
    class BSet:
        pass

    def mkset(t, big):
        S = BSet()
        f = {n: sb(f"f{t}_" + n, (128, 128)) for n in
             ("sg", "a", "logp", "kk", "t1", "ep", "em", "epp", "pls", "RqT", "rkr")}
        f["KN"] = sb(f"f{t}_KN", (128, 2, 128))
        f["KH"] = sb(f"f{t}_KH", (128, 2, 128), MX)
        f["KS"] = sb(f"f{t}_KS", (128, 2, 128), MX)
        nc_max = 16 if big else 2
        S.rw = dict(f=f, X2=sb(f"X2{t}", (128, 2, 128), MX), pLt=sb(f"pLt{t}", (128, 16)),
                    ATk=sb(f"ATk{t}", (128, 2, 256), MX), ATb=sb(f"ATb{t}", (128, 2, 256), MX),
                    NMa=sb(f"NMa{t}", (128, 2, 2, 128), MX), NMb=sb(f"NMb{t}", (128, 2, 2, 128), MX),
                    Xa=sb(f"Xa{t}", (128, 256), MX), Xb=sb(f"Xb{t}", (128, 256), MX),
                    KBt=sb(f"KBt{t}", (128, 2, 128), MX), Yi=sb(f"Yi{t}", (128, 128)),
                    mk4=[sb(f"mk{t}_{i}", (128, 2 * (4 if big else 2) * 65), MX) for i in range(4)],
                    RqTm=sb(f"RqTm{t}", (128, (4 if big else 2) * 128)),
                    PhiT=sb(f"PhiT{t}", (128, nc_max * 64)), Psi=sb(f"Psi{t}", (128, nc_max * 64)),
                    tmpH=sb(f"tmpH{t}", (128, 65)))
        S.ml = dict(ktok=sb(f"ktok{t}", (128, 128), MX), kw=sb(f"kw{t}", (128, 128), MX),
                    Wt=sb(f"Wt{t}", (128, 2, 128)), PT=sb(f"PT{t}", (128, 2, 128), MX),
                    hnd=sb(f"hnd{t}", (128, 2, 65)), st16=sb(f"st16{t}", (128, 16)))
        return S
    SETS = [mkset(0, True), mkset(1, False)]
    for S_ in SETS:
        kb.memset(S_.rw["RqTm"][:], 0.0)
    twb = sb("f_tw", (128, 128))
    identb = sb("identb", (128, 128), MX)
    kb.cp(identb[:], ident[:])
    Vt = sb("Vt", (128, 512), MX)
    Yall = OA
    bon = sb("bon", (128, 8))
    Hs_in = sb("Hs_in", (128, 16, 64))
    H0s = sb("H0s", (128, 16, 65))
    Hn = sb("Hn", (128, 16, 65))
    So = Hs_in
    st8 = sb("st8", (128, 8, 4))
    st8b = sb("st8b", (128, 8, 4))
    g8 = {n: sb("g_" + n, (8, 128)) for n in ("sp", "Bn", "a", "M", "t")}
    G4 = sb("G4", (8, 4, 128))
    negM = sb("negM", (8, 128))
    m0T = sb("m0T", (8, 16))
    dec8 = sb("dec8", (8, 16))
    ddec = sb("ddec", (8, 16, 8))
    decsel = sb("decsel", (128, 16, 4))
    mT8 = sb("mT8", (8, 16))
    gtok = sb("gtok", (128, 4, 8))
    cacc = H0s[:].rearrange("p s c -> p (s c)")[:, 0:1024].rearrange("p (k t) -> p k t", t=128)
    ctmp = Hn[:].rearrange("p s c -> p (s c)")[:, 0:1024].rearrange("p (k t) -> p k t", t=128)
    rtmp = PSs[:, 0:4, :]
    vext = sb("vext", (128, 8, 65), MX)
    Cpb = [sb(f"Cpb{i}", (128, 65), MX) for i in range(4)]
    for i in range(4):
        kb.memset(Cpb[i][:], 0.0)
    Hb = OB[:].rearrange("p (h i) -> p h i", i=64)
    oTs = [sb("oT", (128, 8, 128), BF16),
           gate_s[:].bitcast(BF16)[:, 0:1024].rearrange("p (k t) -> p k t", t=128)]
    SGa = Hs_in[:].rearrange("p s j -> p (s j)")
    MGb = sb("MGb", (128, 1024), BF16)
    mgT = sb("mgT", (128, 8, 128), BF16)
    sh0r = XY[0:16, 0:SHIFT_W]
    cv0r = XY[0:48, 0:1024]

    kb.memset(vext[:, :, 64:65], 1.0)

    def rstd_from(out, in_, scale, eps_ap):
        kb.actf(out, in_, AF.Ln, bias=eps_ap, scale=scale)
        kb.actf(out, out, AF.Exp, scale=-0.5)

    def il(gens):
        gens = list(gens)
        while gens:
            for g_ in list(gens):
                try:
                    next(g_)
                except StopIteration:
                    gens.remove(g_)
            yield

    def run_il(gens):
        for _ in il(gens):
            pass

    def run_il_w(gens_w):
        gens_w = list(gens_w)
        while gens_w:
            for item in list(gens_w):
                g_, w_ = item
                for _ in range(w_):
                    try:
                        next(g_)
                    except StopIteration:
                        gens_w.remove(item)
                        break

    def until_pre(g_, st):
        while not st["rw_pre"]:
            try:
                next(g_)
            except StopIteration:
                return
            yield

    def tile_body(n, sample, xs):
        L = 8 if sample else 64
        NC = 128 // L
        nm = "s" if sample else "p"
        mA, mB, mN, rstm = cs["mA" + nm], cs["mB" + nm], cs["mN" + nm], cs["rst" + nm]
        rowm = cs["rowm" + nm]
        nlev = 3 if sample else 6
        x_t = xt[xs]
        hT, ssq, diag = hTs[xs], ssqs[xs], diags[xs]
        oT = oTs[xs]
        stage = {"rw_pre": False}

        def rstd_from(out, in_, scale, eps_ap):
            kb.actf(out, in_, AF.Ln, bias=eps_ap, scale=scale)
            kb.actf(out, out, AF.Exp, scale=-0.5)

        def g_head():
            kb.memset(ssq[:], 0.0)
            kb.actf(hTf[:].rearrange("p k t -> p (k t)"), x_t[:], AF.Square, accum=ssq[:, 0:1])
            rstd_from(ssq[:, 1:2], ssq[:, 0:1], 1.0 / D, epsb[:, 0:1])
            kb.ts(diag[:], ident[:], ssq[:, 1:2], None, ALU.mult)
            for half in range(2):
                pb = pbank()
                for k4 in range(4):
                    k = half * 4 + k4
                    kb.mm(pb[:, k4 * 128:(k4 + 1) * 128], x_t[:, k * 128:(k + 1) * 128], diag[:])
                if sample:
                    kb.acp(hTf[:, half * 4:half * 4 + 4, :].rearrange("p k t -> p (k t)"), pb[:, :])
                else:
                    kb.tt(hTf[:, half * 4:half * 4 + 4, :], pb[:, :].rearrange("p (k t) -> p k t", t=128),
                          A_T[:, half * 4:half * 4 + 4, 0:1].to_broadcast([128, 4, 128]), ALU.mult)
                yield
            if sample:
                Ab = A_T[:, :, 1:17].unsqueeze(3).to_broadcast([128, 8, 16, 8])
                Bb = B_T[:, :, 1:17].unsqueeze(3).to_broadcast([128, 8, 16, 8])
                hv = hTf[:].rearrange("p k (s t) -> p k s t", t=8)
                kb.tt(hv, hv, Ab, ALU.mult)
                kb.tt(hT[:].rearrange("p k (s t) -> p k s t", t=8), hv, Bb, ALU.add)
            else:
                kb.tt(hT[:], hTf[:], B_T[:, :, 0:1].to_broadcast([128, 8, 128]), ALU.add)
            yield

        def inproj_fm(dst_fn, col0, nchunks, sv=None):
            for c0 in range(0, nchunks, 4):
                n4 = min(4, nchunks - c0)
                wp = wpiece(col0 + c0 * 128, n4 * 128)
                pb = pbank()
                for cc in range(n4):
                    for k in range(8):
                        kb.mm(pb[:, cc * 128:(cc + 1) * 128], wp[:, k, cc * 128:(cc + 1) * 128], hT[:, k, :],
                              start=(k == 0), stop=(k == 7))
                src = pb[:, 0:n4 * 128].rearrange("p (c t) -> p c t", t=128)
                kb.acp(dst_fn(c0, n4), sv(src) if sv else src)
                yield

        def inproj_tm(gi, col):
            wp = wpiece(col, 512)
            pb = pbank()
            for k in range(8):
                kb.mm(pb[:, :], hT[:, k, :], wp[:, k, :], start=(k == 0), stop=(k == 7))
            kb.acp(TM[:, gi, :], pb[:, :])
            yield

        G = min(NC, 4)

        def il(gens):
            gens = list(gens)
            while gens:
                for g_ in list(gens):
                    try:
                        next(g_)
                    except StopIteration:
                        gens.remove(g_)
                yield

        def run_il(gens):
            for _ in il(gens):
                pass

        def per_seq_tiled(lhs, rhs, evac):
            for s0 in (0, 8):
                pp = ppair()
                for s in range(s0, s0 + 8):
                    for e in range(2):
                        sl = slice(64 * e, 64 * e + 64)
                        kb.mm(pp[sl, e, (s - s0) * 64:(s - s0) * 64 + 64], lhs(s, sl), rhs(s, sl))
                for e in range(2):
                    sl = slice(64 * e, 64 * e + 64)
                    evac(sl, s0, pp[sl, e, :].rearrange("p (s i) -> p s i", i=64))

        def heads_T(src_fn, dst):
            pp = ppair()
            for hp_ in range(4):
                for e in range(2):
                    sl = slice(64 * e, 64 * e + 64)
                    kb.mm(pp[sl, e, hp_ * 64:hp_ * 64 + 64], src_fn(hp_, sl), ident[sl, sl])
            for e in range(2):
                sl = slice(64 * e, 64 * e + 64)
                kb.cp(dst[sl, :, :], pp[sl, e, 0:256].rearrange("p (h j) -> p h j", j=64))

        def rq_group(src_ap, g0, RqTm):
            if sample:
                kb.memset(RqTm[:], 0.0, e=kb.pool)
            W_ = RqTm[:].shape[1]
            dst = bass.AP(RqTm, g0 * L, [[W_, 128], [128 + L, G], [1, L]])
            kb.cp(dst, src_ap[:, g0 * L:(g0 + G) * L].rearrange("p (c l) -> p c l", l=L))
            return RqTm[:, 0:G * 128].rearrange("p (c t) -> p c t", t=128)

        def g_rwkv():
            yield from inproj_fm(lambda c0, n4: PSb[:, c0:c0 + n4, :], 0, 13)
            yield from inproj_tm(0, C_ZA)
            if sample:
                sv4 = PSs[:].rearrange("p c (s t) -> p c s t", t=8)
                bv = PSb[:].rearrange("p c (s t) -> p c s t", t=8)
                kb.tt(sv4[:, :, :, 1:8], bv[:, :, :, 0:7], bv[:, :, :, 1:8], ALU.subtract)
                kb.dma(sh0r, shift0)
                pb = pbank()
                pb2 = pbank()
                for c in range(13):
                    tgt = pb if c < 8 else pb2
                    cc = c % 8
                    kb.mm(tgt[:, cc * 16:(cc + 1) * 16], sh0r[:, c * 128:(c + 1) * 128], ident[0:16, 0:16])
                kb.tt(sv4[:, 0:8, :, 0], pb[:, 0:128].rearrange("p (c s) -> p c s", s=16), bv[:, 0:8, :, 0], ALU.subtract)
                kb.tt(sv4[:, 8:13, :, 0], pb2[:, 0:80].rearrange("p (c s) -> p c s", s=16), bv[:, 8:13, :, 0], ALU.subtract)
            else:
                kb.tt(PSs[:, :, 1:128], PSb[:, :, 0:127], PSb[:, :, 1:128], ALU.subtract)
                kb.tt(PSs[:, :, 0], carry_ps[:], PSb[:, :, 0], ALU.subtract)
                kb.cp(carry_ps[:], PSb[:, :, 127], e=kb.pool)
            yield
            kb.tt(PSs[:], PSs[:], muT.unsqueeze(2).to_broadcast([128, 13, 128]), ALU.mult)
            yield
            kb.tt(PSs[:], PSs[:], PSb[:], ALU.add)
            if sample:
                scr = XY[:, 1700:1908].rearrange("p (c s) -> p c s", s=16)
                kb.cp(scr, PSb[:].rearrange("p c (s t) -> p c s t", t=8)[:, :, :, 7])
                for c0 in range(0, 13, 4):
                    n4 = min(4, 13 - c0)
                    pb = pbank()
                    for cc in range(n4):
                        kb.mm(pb[0:16, cc * 128:(cc + 1) * 128], scr[:, c0 + cc, :], ident[:])
                    kb.cp(XY[0:16, c0 * 128:(c0 + n4) * 128], pb[0:16, 0:n4 * 128])
                kb.dma(sshift_o, XY[0:16, 0:SHIFT_W])
            elif n == last_prompt:
                kb.dma(pshift_o.rearrange("(c p) -> p c", p=128), PSb[:, :, 127], allow_slow_non_contiguous=True)
            pb = pbank()
            for hp in range(4):
                kb.mm(pb[:, hp * 128:(hp + 1) * 128], PSs[:, 8 + hp, :], ident[:])
            kb.acp(Vt[:], pb[:, :])
            kb.actf(twb[0:64, :], PSs[0:64, 12, :], AF.Tanh)

            stage["rw_pre"] = True

            def rwkv_hp(hp, S):
                R_ = S.rw
                f, X2, pLt, ATk, ATb, NMa, NMb = R_["f"], R_["X2"], R_["pLt"], R_["ATk"], R_["ATb"], R_["NMa"], R_["NMb"]
                Xa, Xb, KBt, Yi, mk4, RqTm = R_["Xa"], R_["Xb"], R_["KBt"], R_["Yi"], R_["mk4"], R_["RqTm"]
                PhiT, Psi, tmpH = R_["PhiT"], R_["Psi"], R_["tmpH"]
                r_, k_ = PSs[:, hp, :], PSs[:, 4 + hp, :]
                KN = f["KN"]
                kb.ts(f["kk"][:], k_, kkT[:, hp:hp + 1], None, ALU.mult)
                kb.tt(f["epp"][:], f["kk"][:], f["kk"][:], ALU.mult)
                pp = ppair()
                kb.mm(pp[:, 0, 0:128], wlora[0:64, hp * 128:(hp + 1) * 128], twb[0:64, :])
                kb.mm(pp[:, 1, 0:128], wlora[64:128, hp * 128:(hp + 1) * 128], PSs[64:128, 12, :])
                pb = pbank()
                kb.mm(pb[:, 0:128], cs["blockones"][:], f["epp"][:])
                kb.actf(f["sg"][:], pp[:, 0, 0:128], AF.Sigmoid, bias=w0T[:, hp:hp + 1])
                kb.actf(f["a"][:], pp[:, 1, 0:128], AF.Sigmoid, bias=a0T[:, hp:hp + 1])
                kb.actf(f["epp"][:], pb[:, 0:128], AF.Ln, bias=epsb[:, 3:4])
                kb.actf(f["epp"][:], f["epp"][:], AF.Exp, scale=-0.5)
                kb.scan(f["logp"][:], rstm[:], f["sg"][:], 0.0, ALU.mult, ALU.add)
                kb.ts(f["t1"][:], f["a"][:], kaT[:, hp:hp + 1], omkaT[:, hp:hp + 1], ALU.mult, ALU.add)
                kb.tt(KN[:, 0, :], k_, f["t1"][:], ALU.mult)
                kb.tt(f["kk"][:], f["kk"][:], f["epp"][:], ALU.mult)
                kb.stt(KN[:, 1, :], f["kk"][:], -1.0, f["a"][:], ALU.mult, ALU.mult)
                kb.stt(f["rkr"][:], r_, rkT[:, hp:hp + 1], KN[:, 0, :], ALU.mult, ALU.mult)
                pb = pbank()
                kb.mm(pb[:, 0:2], f["rkr"][:], cs["blocksel"][:])
                kb.cp(bon[:, 2 * hp:2 * hp + 2], pb[:, 0:2])
                yield
                lp3 = f["logp"][:].rearrange("p (c l) -> p c l", l=L)
                kb.tt(f["t1"][:], f["logp"][:], f["sg"][:], ALU.subtract)
                kb.tt(f["rkr"][:].rearrange("p (c l) -> p c l", l=L), lp3[:, :, L - 1:L].to_broadcast([128, NC, L]), lp3,
                      ALU.subtract)
                kb.actf(f["ep"][:], f["logp"][:], AF.Exp, scale=-EXPM05)
                kb.actf(f["epp"][:], f["t1"][:], AF.Exp, scale=-EXPM05)
                kb.actf(f["em"][:], f["logp"][:], AF.Exp, scale=EXPM05)
                kb.actf(f["pls"][:], f["rkr"][:], AF.Exp, scale=-EXPM05)
                kb.tt(X2[:, 1, :], r_, f["ep"][:], ALU.mult)
                kb.tt(X2[:, 0, :], f["kk"][:], f["epp"][:], ALU.mult)
                kb.tt(f["KH"][:], KN[:], f["em"][:].unsqueeze(1).to_broadcast([128, 2, 128]), ALU.mult)
                kb.cp(pLt[:, 0:NC], f["ep"][:].rearrange("p (c l) -> p c l", l=L)[:, :, L - 1])
                kb.tt(f["KS"][:], KN[:], f["pls"][:].unsqueeze(1).to_broadcast([128, 2, 128]), ALU.mult)
                yield
                pb = pbank()
                kb.mm(pb[:, 0:128], f["KS"][:, 0, :], identb[:])
                kb.mm(pb[:, 128:256], f["KS"][:, 1, :], identb[:])
                kb.acp(KBt[:].rearrange("p a j -> p (a j)"), pb[:, 0:256])
                if not sample:
                    rm0 = rowm[:, 0:G].unsqueeze(1).unsqueeze(3).to_broadcast([128, 2, G, 64])
                    for i4, s_ in ((0, KBt[:, 1, :].rearrange("p (e j) -> p e j", e=2)),
                                   (2, Vt[:, hp * 128:(hp + 1) * 128].rearrange("p (e j) -> p e j", e=2))):
                        kb.tt(mk4[i4][:, 0:2 * G * 64].rearrange("p (e c j) -> p e c j", e=2, j=64),
                              s_.unsqueeze(2).to_broadcast([128, 2, G, 64]), rm0, ALU.mult, e=kb.pool)
                yield
                pk, pbb = ppair(), ppair()
                X2f = X2[:].rearrange("p a t -> p (a t)")
                for e in range(2):
                    sl = slice(64 * e, 64 * e + 64)
                    kb.mm(pk[:, e, 0:256], f["KH"][sl, 0, :], X2f[sl, :])
                    kb.mm(pbb[:, e, 0:256], f["KH"][sl, 1, :], X2f[sl, :])
                    kb.mm(pk[:, e, 256:384], X2[sl, 0, :], f["KH"][sl, 1, :])
                kb.tt(ATk[:], pk[:, :, 0:256], mA[:].unsqueeze(1).to_broadcast([128, 2, 256]), ALU.mult)
                kb.tt(ATb[:], pbb[:, :, 0:256], mA[:].unsqueeze(1).to_broadcast([128, 2, 256]), ALU.mult)
                kb.tt(NMa[:, 0, :, :], pk[:, :, 256:384], mN[:].unsqueeze(1).to_broadcast([128, 2, 128]), ALU.mult)
                kb.cp(NMa[:, 1, :, :], ATb[:, :, 0:128], e=kb.pool)
                yield
                pp = ppair()
                for e in range(2):
                    kb.mm(pp[:, e, 0:64], ATk[:, e, 0:128], Vt[:, hp * 128 + 64 * e: hp * 128 + 64 * e + 64])
                for e in range(2):
                    sl = slice(64 * e, 64 * e + 64)
                    kb.mm(pp[:, e, 64:128], X2[sl, 0, :], identb[sl, sl])
                kb.cp(Xa[:].rearrange("p (e c) -> p e c", e=2), pp[:, :, 0:128])
                yield
                cur, nxt = NMa, NMb
                xc, xn = Xa, Xb
                for lev in range(nlev):
                    pa = pbank()
                    for e in range(2):
                        kb.mm(pa[:, 128 * e:128 * e + 128], cur[:, 1, e, :], xc[:, 128 * e:128 * e + 128])
                    if lev < nlev - 1:
                        pp = pbank()
                        pp4 = pp[:, :].rearrange("p (a e c) -> p a e c", a=2, e=2)
                        for e in range(2):
                            kb.mm(pp4[:, 0, e, :], cur[:, 1, e, :], cur[:, 0, e, :])
                            kb.mm(pp4[:, 1, e, :], cur[:, 0, e, :], cur[:, 1, e, :])
                        kb.acp(nxt[:].rearrange("p a e c -> p (a e c)"), pp[:, :])
                    kb.tt(xn[:], xc[:], pa[:, 0:256], ALU.add)
                    cur, nxt = nxt, cur
                    xc, xn = xn, xc
                    yield
                Xf = xc
                pb = pbank()
                for e in range(2):
                    kb.mm(pb[64 * e:64 * e + 64, 0:128], Xf[:, 128 * e + 64:128 * e + 128], ATb[:, e, 128:256])
                kb.tt(f["RqT"][:], X2[:, 1, :], pb[:, 0:128], ALU.add)
                rq_p = None if sample else rq_group(f["RqT"], 0, RqTm)
                yield
                pb = pbank()
                for e in range(2):
                    kb.mm(pb[:, 64 * e:64 * e + 64], ATk[:, e, 128:256], Vt[:, hp * 128 + 64 * e: hp * 128 + 64 * e + 64],
                          start=True, stop=False)
                    kb.mm(pb[:, 64 * e:64 * e + 64], ATb[:, e, 128:256], Xf[:, 128 * e:128 * e + 64], start=False, stop=True)
                kb.acp(Yi[:], pb[:, 0:128])
                yield
                srcs = (KBt[:, 1, :].rearrange("p (e j) -> p e j", e=2), KBt[:, 0, :].rearrange("p (e j) -> p e j", e=2),
                        Vt[:, hp * 128:(hp + 1) * 128].rearrange("p (e j) -> p e j", e=2),
                        Xf[:].rearrange("p (e c) -> p e c", e=2)[:, :, 0:64])
                for g0 in range(0, NC, G):
                    rm = rowm[:, g0:g0 + G].unsqueeze(1).unsqueeze(3).to_broadcast([128, 2, G, 64])
                    mv = [mk4[i4][:, 0:2 * G * 64].rearrange("p (e c j) -> p e c j", e=2, j=64) for i4 in range(4)]
                    for i4, s_ in enumerate(srcs):
                        if i4 == 1 or (not sample and i4 != 3):
                            continue
                        kb.tt(mv[i4], s_.unsqueeze(2).to_broadcast([128, 2, G, 64]), rm, ALU.mult,
                              e=(None if i4 == 3 else kb.pool))
                    for e in range(2):
                        sl = slice(64 * e, 64 * e + 64)
                        me = [mk4[i4][:, e * G * 64:(e + 1) * G * 64] for i4 in range(4)]
                        pb = pbank()
                        kb.mm(pb[sl, 0:G * 64], Xf[:, 128 * e + 64:128 * e + 128], me[0])
                        kb.acp(PhiT[sl, g0 * 64:(g0 + G) * 64], pb[sl, 0:G * 64])
                        pb = pbank()
                        kb.mm(pb[sl, 0:G * 64], KBt[:, 0, sl], me[2], start=True, stop=False)
                        kb.mm(pb[sl, 0:G * 64], KBt[:, 1, sl], me[3], start=False, stop=True)
                        kb.cp(Psi[sl, g0 * 64:(g0 + G) * 64], pb[sl, 0:G * 64])
                    yield
                PhiT3 = PhiT[:, 0:NC * 64].rearrange("p (c j) -> p c j", j=64)
                Psi3 = Psi[:, 0:NC * 64].rearrange("p (c j) -> p c j", j=64)

                if sample:
                    for e in range(2):
                        kb.dma(Hs_in[64 * e:64 * e + 64, :, :], S0[:, 2 * hp + e, :, :].rearrange("s i j -> i s j"))
                    per_seq_tiled(lambda s, sl: Hs_in[sl, s, :], lambda s, sl: ident[sl, sl],
                                  lambda sl, s0, v: kb.cp(H0s[sl, s0:s0 + 8, 0:64], v))
                    kb.tt(Hn[:, :, 0:64], H0s[:, :, 0:64], pLt[:, 0:16].unsqueeze(2).to_broadcast([128, 16, 64]), ALU.mult)
                    kb.tt(Hn[:, :, 0:64], Hn[:, :, 0:64], Psi3, ALU.add)
                    per_seq_tiled(lambda s, sl: PhiT3[sl, s, :], lambda s, sl: H0s[sl, s, 0:64],
                                  lambda sl, s0, v: kb.tt(Hn[sl, s0:s0 + 8, 0:64], Hn[sl, s0:s0 + 8, 0:64], v, ALU.add))
                    for g0 in range(0, 16, G):
                        rq = rq_group(f["RqT"], g0, RqTm)
                        pq = ppair()
                        for e in range(2):
                            sl = slice(64 * e, 64 * e + 64)
                            for g_ in range(G):
                                kb.mm(pq[:, e, 0:64], rq[sl, g_, :], H0s[sl, g0 + g_, 0:64], start=(g_ == 0), stop=(g_ == G - 1))
                        ydst = Yall[:, hp * 128:(hp + 1) * 128] if g0 + G == 16 else Yi[:]
                        kb.tt(ydst.rearrange("p (e i) -> p e i", e=2), Yi[:].rearrange("p (e i) -> p e i", e=2), pq[:, :, 0:64], ALU.add)
                    per_seq_tiled(lambda s, sl: Hn[sl, s, 0:64], lambda s, sl: ident[sl, sl],
                                  lambda sl, s0, v: kb.cp(So[sl, s0:s0 + 8, :], v))
                    for e in range(2):
                        kb.dma(sS_o[:, 2 * hp + e, :, :].rearrange("s i j -> i s j"), So[64 * e:64 * e + 64, :, :])
                else:
                    rq = rq_p
                    for c in range(NC):
                        kb.stt(tmpH[:, 0:64], Hst[hp][:, :], pLt[:, c:c + 1], Psi3[:, c, :], ALU.mult, ALU.add)
                        pp, pq = ppair(), ppair()
                        for e in range(2):
                            sl = slice(64 * e, 64 * e + 64)
                            kb.mm(pp[sl, e, 0:64], PhiT3[sl, c, :], Hst[hp][sl, :])
                            kb.mm(pq[:, e, 0:64], rq[sl, c, :], Hst[hp][sl, :])
                        for e in range(2):
                            sl = slice(64 * e, 64 * e + 64)
                            kb.tt(Hst[hp][sl, :], tmpH[sl, 0:64], pp[sl, e, 0:64], ALU.add)
                        ydst = Yall[:, hp * 128:(hp + 1) * 128] if c == NC - 1 else Yi[:]
                        kb.tt(ydst.rearrange("p (e i) -> p e i", e=2), Yi[:].rearrange("p (e i) -> p e i", e=2), pq[:, :, 0:64], ALU.add)
                        yield

            if sample:
                for hp_ in range(4):
                    yield from il([rwkv_hp(hp_, SETS[0])])
            else:
                yield from il([rwkv_hp(0, SETS[0]), rwkv_hp(1, SETS[1])])
                yield from il([rwkv_hp(2, SETS[0]), rwkv_hp(3, SETS[1])])

            Y3 = Yall[:].rearrange("p (h i) -> p h i", i=64)
            sgz = PSs[:, 4:8, :].rearrange("p a t -> p (a t)")
            kb.actf(sgz, TM[:, 0, :], AF.Sigmoid)
            kb.tt(sgz, sgz, TM[:, 0, :], ALU.mult, e=kb.pool)
            g2 = PSs[:, 8:12, :].rearrange("p a t -> p (a t)")
            kb.tt(g2.rearrange("p (h i) -> p h i", i=64), Vt[:].rearrange("p (h i) -> p h i", i=64),
                  bon[:].unsqueeze(2).to_broadcast([128, 8, 64]), ALU.mult, e=kb.pool)
            kb.tt(g2, g2, lnb_b, ALU.add, e=kb.pool)
            kb.tt(g2, g2, sgz, ALU.mult, e=kb.pool)
            kb.tt(sgz, sgz, lnw_b, ALU.mult, e=kb.pool)
            yield
            kb.rsum(st8[:, :, 0], Y3)
            yield
            kb.ts(st8[:, :, 0], st8[:, :, 0], -1.0 / 64, None, ALU.mult)
            yield
            kb.tt(Y3, Y3, st8[:, :, 0:1].to_broadcast([128, 8, 64]), ALU.add)
            yield
            kb.tt(rtmp.rearrange("p a (b i) -> p (a b) i", i=64), Y3, Y3, ALU.mult)
            yield
            kb.rsum(st8[:, :, 1], rtmp.rearrange("p a (b i) -> p (a b) i", i=64))
            yield
            rstd_from(st8[:, :, 2], st8[:, :, 1], 1.0 / 64, epsb[0:128, 1:2])
            yield
            kb.tt(Y3, Y3, st8[:, :, 2:3].to_broadcast([128, 8, 64]), ALU.mult)
            yield
            kb.tt(Yall[:], Yall[:], sgz, ALU.mult)
            yield
            kb.tt(OAb[:], Yall[:], g2, ALU.add)

        def g_mlstm():
            while not stage["rw_pre"]:
                yield
            yield from inproj_tm(1, C_VB)
            yield from inproj_tm(2, C_OB)
            yield from inproj_tm(3, C_ZB)
            if sample:
                qv = QKb[:].rearrange("p c (s u) -> p c s u", u=11)
                kb.dma(cv0r[:], conv0)
                pb = pbank()
                for c in range(8):
                    kb.mm(pb[:, c * 48:(c + 1) * 48], cv0r[:, c * 128:(c + 1) * 128], ident[0:48, 0:48])
                kb.cp(qv[:, :, :, 0:3], pb[:, 0:384].rearrange("p (c s u) -> p c s u", c=8, u=3))
                yield from inproj_fm(lambda c0, n4: qv[:, c0:c0 + n4, :, 3:11], C_QK, 8,
                          sv=lambda a: a.rearrange("p c (s u) -> p c s u", u=8))
                scr = XY[:, 1600:1984].rearrange("p (c u s) -> p c u s", u=3, s=16)
                kb.cp(scr, qv[:, :, :, 8:11].rearrange("p c s u -> p c u s"))
                for u in range(3):
                    pp = ppair()
                    for c in range(8):
                        kb.mm(pp[0:16, c // 4, (c % 4) * 128:(c % 4) * 128 + 128], scr[:, c, u, :], ident[:])
                    kb.cp(XY[0:16, 0:1024].rearrange("p (b c) -> p b c", b=2), pp[0:16, :, :])
                    kb.dma(sconv_o[:, u, :], XY[0:16, 0:1024])
                taps = [qv[:, :, :, j:j + 8] for j in range(4)]
                shp = [128, 8, 16, 8]
                cv = lambda t_: t_[:].rearrange("p c (s u) -> p c s u", u=8)
                wb = lambda j: pconv[:, j * 8:(j + 1) * 8].unsqueeze(2).unsqueeze(3).to_broadcast(shp)
                cbb = pconv[:, 32:40].unsqueeze(2).unsqueeze(3).to_broadcast(shp)
            else:
                kb.cp(QKb[:, :, 0:3], carry_qk[:])
                yield from inproj_fm(lambda c0, n4: QKb[:, c0:c0 + n4, 3:131], C_QK, 8)
                kb.cp(carry_qk[:], QKb[:, :, 128:131], e=kb.pool)
                if n == last_prompt:
                    for c in range(8):
                        kb.dma(pconv_o[:, c * 128:(c + 1) * 128].rearrange("u p -> p u"), QKb[:, c, 128:131],
                               allow_slow_non_contiguous=True)
                taps = [QKb[:, :, j:j + 128] for j in range(4)]
                shp = [128, 8, 128]
                cv = lambda t_: t_[:]
                wb = lambda j: pconv[:, j * 8:(j + 1) * 8].unsqueeze(2).to_broadcast(shp)
                cbb = pconv[:, 32:40].unsqueeze(2).to_broadcast(shp)
            yield
            kb.tt(cv(cacc), taps[0], wb(0), ALU.mult)
            for j in range(1, 4):
                kb.tt(cv(ctmp), taps[j], wb(j), ALU.mult, e=kb.pool)
                kb.tt(cv(cacc), cv(cacc), cv(ctmp), ALU.add)
            yield
            kb.tt(cv(cacc), cv(cacc), cbb, ALU.add)
            yield
            kb.actf(ctmp[:], cacc[:], AF.Sigmoid)
            yield
            kb.tt(QK[:, 0:4, :], cacc[:, 0:4, :], ctmp[:, 0:4, :], ALU.mult)
            yield
            kb.stt(QK[:, 4:8, :], cacc[:, 4:8, :], 0.125, ctmp[:, 4:8, :], ALU.mult, ALU.mult)
            pgi, pgf = pbank(), pbank()
            wpg = wpiece(C_I, 16)
            for k in range(8):
                kb.mm(pgi[0:8, 0:128], wpg[:, k, 0:8], hT[:, k, :], start=(k == 0), stop=(k == 7))
            for k in range(8):
                kb.mm(pgf[0:8, 0:128], wpg[:, k, 8:16], hT[:, k, :], start=(k == 0), stop=(k == 7))
            g = g8
            kb.actf(g["sp"][:], pgf[0:8, 0:128], AF.Exp, bias=pgate[:, 1:2], scale=-1.0)
            kb.actf(g["sp"][:], g["sp"][:], AF.Ln, bias=epsb[0:8, 2:3])
            NCm = 16 if sample else 1
            Lm = 128 // NCm
            if sample:
                kb.dma(m0T[:], m0.rearrange("s h -> h s"), allow_slow_non_contiguous=True)
                kb.scan(g["Bn"][:], cs["rst8m"][:], g["sp"][:], 0.0, ALU.mult, ALU.add)
            else:
                kb.scan(g["Bn"][:], cs["ones8t"][:], g["sp"][:], Bn_c[:, 0:1], ALU.mult, ALU.add)
            kb.stt(g["a"][:], pgi[0:8, 0:128], pgate[:, 0:1], g["Bn"][:], ALU.add, ALU.add)
            if sample:
                a3 = g["a"][:].rearrange("h (s u) -> h s u", u=8)
                kb.cp(g["t"][:], g["a"][:])
                t3 = g["t"][:].rearrange("h (s u) -> h s u", u=8)
                kb.tt(t3[:, :, 0], a3[:, :, 0], m0T[:], ALU.max)
                kb.scan(g["M"][:], cs["rst8a"][:], g["t"][:], 0.0, ALU.add, ALU.max)
                M3 = g["M"][:].rearrange("h (s u) -> h s u", u=8)
                Mprev_b = m0T[:].unsqueeze(2).to_broadcast([8, 16, 8])
                Mend = M3[:, :, 7]
            else:
                kb.scan(g["M"][:], cs["zeros8t"][:], g["a"][:], M_c[:, 0:1], ALU.add, ALU.max)
                M3 = g["M"][:].rearrange("h (s u) -> h s u", u=128)
                Mprev_b = M_c[:, 0:1].unsqueeze(2).to_broadcast([8, 1, 128])
                Mend = M3[:, :, 127]
            G43 = lambda q: G4[:, q, :].rearrange("h (s u) -> h s u", u=Lm)
            yield
            kb.cp(G4[:, 0, :], g["a"][:])
            yield
            kb.tt(G43(1), Mprev_b, M3, ALU.subtract)
            yield
            kb.actf(G4[:, 1, :], G4[:, 1, :], AF.Exp)
            yield
            kb.tt(G4[:, 2, :], g["Bn"][:], g["M"][:], ALU.subtract)
            yield
            kb.actf(G4[:, 2, :], G4[:, 2, :], AF.Exp)
            yield
            kb.tt(G43(3), a3 if sample else g["a"][:].rearrange("h (s u) -> h s u", u=128),
                  Mend.unsqueeze(2).to_broadcast([8, NCm, Lm]), ALU.subtract)
            yield
            kb.actf(G4[:, 3, :], G4[:, 3, :], AF.Exp)
            yield
            kb.ts(negM[:], g["M"][:], -1.0, None, ALU.mult)
            if sample:
                kb.tt(dec8[:, 0:16], m0T[:], Mend, ALU.subtract)
            else:
                kb.tt(dec8[:, 0:1], M_c[:, 0:1], Mend, ALU.subtract)
            yield
            kb.actf(dec8[:, 0:NCm], dec8[:, 0:NCm], AF.Exp)
            Bn3 = g["Bn"][:].rearrange("h (s u) -> h s u", u=Lm)
            yield
            kb.tt(mT8[:, 0:NCm], Mend, Bn3[:, :, Lm - 1], ALU.subtract)
            if sample:
                kb.dma(sm_o.rearrange("s h -> h s"), mT8[:, 0:16], allow_slow_non_contiguous=True)
            else:
                if n == last_prompt:
                    kb.dma(pm_o, mT8[:, 0:1])
                kb.cp(Bn_c[:], g["Bn"][:, 127:128], e=kb.pool)
                kb.cp(M_c[:], g["M"][:, 127:128], e=kb.pool)
            pb = pbank()
            for q in range(4):
                kb.mm(pb[:, q * 8:(q + 1) * 8], G4[:, q, :], cs["i8"][:])
            kb.cp(gtok[:].rearrange("p q h -> p (q h)"), pb[:, 0:32])
            kb.tt(ddec[:, 0:NCm, :], dec8[:, 0:NCm].unsqueeze(2).to_broadcast([8, NCm, 8]),
                  cs["i8"][:].unsqueeze(1).to_broadcast([8, NCm, 8]), ALU.mult)
            pb = pbank()
            kb.mm(pb[:, 0:NCm * 8], cs["ones8"][:], ddec[:, 0:NCm, :].rearrange("h c k -> h (c k)"))
            pv3 = pb[:, 0:NCm * 8].rearrange("p (c hp e) -> p c hp e", hp=4, e=2)
            kb.cp(decsel[0:64, 0:NCm, :], pv3[0:64, :, :, 0])
            kb.cp(decsel[64:128, 0:NCm, :], pv3[64:128, :, :, 1])
            kb.cp(vext[:, :, 0:64], TM[:, 1, :].rearrange("p (h i) -> p h i", i=64))
            mbias = cs["mbs"] if sample else cs["mbp"]
            def mlstm_hp(hp, S):
                M_ = S.ml
                ktok, kw, Wt, PT, hnd, st16 = M_["ktok"], M_["kw"], M_["Wt"], M_["PT"], M_["hnd"], M_["st16"]
                mk4, RqTm = S.rw["mk4"], S.rw["RqTm"]
                pb = pbank()
                kb.mm(pb[:, 0:128], QK[:, 4 + hp, :], identb[:])
                kb.cp(ktok[:], pb[:, 0:128])
                kb.tt(kw[:].rearrange("p (e d) -> p e d", e=2), ktok[:].rearrange("p (e d) -> p e d", e=2),
                      gtok[:, 3, 2 * hp:2 * hp + 2].unsqueeze(2).to_broadcast([128, 2, 64]), ALU.mult)
                pS, pE = ppair(), pbank()
                for e in range(2):
                    sl = slice(64 * e, 64 * e + 64)
                    kb.mm(pS[:, e, 0:128], QK[sl, 4 + hp, :], QK[sl, hp, :])
                for e in range(2):
                    kb.mm(pE[:, e * 128:(e + 1) * 128], cs["selh"][:, 2 * hp + e, :], negM[:], start=True, stop=False)
                    kb.mm(pE[:, e * 128:(e + 1) * 128], ident[:], mbias[:], start=False, stop=True)
                    kb.actf(Wt[:, e, :], pE[:, e * 128:(e + 1) * 128], AF.Exp, bias=gtok[:, 0, 2 * hp + e:2 * hp + e + 1])
                kb.tt(PT[:], pS[:, :, 0:128], Wt[:], ALU.mult)
                yield
                if sample:
                    for e in range(2):
                        kb.dma(Hs_in[64 * e:64 * e + 64, :, :], C0[:, 2 * hp + e, :, :].rearrange("s i j -> i s j"))
                    kb.dma(XY[0:16, 1024:1152], n0.rearrange("s h j -> s (h j)")[:, hp * 128:(hp + 1) * 128])
                    pb = pbank()
                    kb.mm(pb[:, 0:16], XY[0:16, 1024:1152], ident[0:16, 0:16])
                    kb.cp(H0s[:, :, 64], pb[:, 0:16])
                    per_seq_tiled(lambda s, sl: Hs_in[sl, s, :], lambda s, sl: ident[sl, sl],
                                  lambda sl, s0, v: kb.cp(H0s[sl, s0:s0 + 8, 0:64], v))
                if not sample:
                    pJp = ppair()
                    for e in range(2):
                        sl = slice(64 * e, 64 * e + 64)
                        kb.mm(pJp[:, e, 0:65], QK[sl, hp, :], Cpb[hp][sl, :])
                if sample:
                    for g0 in range(0, 16, G):
                        q3 = rq_group(QK[:, hp, :], g0, RqTm)
                        pq = ppair()
                        for e in range(2):
                            sl = slice(64 * e, 64 * e + 64)
                            for g_ in range(G):
                                kb.mm(pq[:, e, 0:65], q3[sl, g_, :], H0s[sl, g0 + g_, :], start=(g_ == 0), stop=(g_ == G - 1))
                        if g0 == 0:
                            kb.cp(hnd[:], pq[:, :, 0:65])
                        else:
                            kb.tt(hnd[:], hnd[:], pq[:, :, 0:65], ALU.add)
                    kb.tt(hnd[:], hnd[:], gtok[:, 1, 2 * hp:2 * hp + 2].unsqueeze(2).to_broadcast([128, 2, 65]), ALU.mult)
                else:
                    kb.tt(hnd[:], pJp[:, :, 0:65],
                          gtok[:, 1, 2 * hp:2 * hp + 2].unsqueeze(2).to_broadcast([128, 2, 65]), ALU.mult)
                yield
                pI = pbank()
                for e in range(2):
                    kb.mm(pI[:, e * 65:(e + 1) * 65], PT[:, e, :], vext[:, 2 * hp + e, :])
                kb.tt(hnd[:], hnd[:], pI[:, 0:130].rearrange("p (e c) -> p e c", e=2), ALU.add)
                kb.stt(st16[:, 0:2], hnd[:, :, 64], -1.0, hnd[:, :, 64], ALU.mult, ALU.max)
                kb.tt(st16[:, 0:2], st16[:, 0:2], gtok[:, 2, 2 * hp:2 * hp + 2], ALU.max)
                kb.recip(st16[:, 2:4], st16[:, 0:2])
                kb.tt(Hb[:, 2 * hp:2 * hp + 2, :], hnd[:, :, 0:64], st16[:, 2:4].unsqueeze(2).to_broadcast([128, 2, 64]), ALU.mult)
                yield
                if sample:
                    kb.tt(Hn[:], H0s[:], decsel[:, 0:16, hp:hp + 1].to_broadcast([128, 16, 65]), ALU.mult)
                    for g0 in range(0, 16, G):
                        pb = pbank()
                        for e in range(2):
                            sl = slice(64 * e, 64 * e + 64)
                            kb.tt(mk4[e][:, 0:G * 65].rearrange("p (c j) -> p c j", j=65),
                                  vext[:, 2 * hp + e, :].unsqueeze(1).to_broadcast([128, G, 65]),
                                  cs["rowms"][:, g0:g0 + G].unsqueeze(2).to_broadcast([128, G, 65]), ALU.mult,
                                  e=(kb.pool if e else None))
                            kb.mm(pb[sl, 0:G * 65], kw[:, sl], mk4[e][:, 0:G * 65])
                        kb.tt(Hn[:, g0:g0 + G, :], Hn[:, g0:g0 + G, :], pb[:, 0:G * 65].rearrange("p (s c) -> p s c", c=65), ALU.add)
                    per_seq_tiled(lambda s, sl: Hn[sl, s, 0:64], lambda s, sl: ident[sl, sl],
                                  lambda sl, s0, v: kb.cp(So[sl, s0:s0 + 8, :], v))
                    for e in range(2):
                        kb.dma(sC_o[:, 2 * hp + e, :, :].rearrange("s i j -> i s j"), So[64 * e:64 * e + 64, :, :])
                    kb.cp(XY[:, 1200:1216], Hn[:, :, 64])
                    pb = pbank()
                    kb.mm(pb[0:16, 0:128], XY[:, 1200:1216], ident[:])
                    kb.cp(XY[0:16, 1536:1664], pb[0:16, 0:128])
                    kb.dma(sn_o.rearrange("s h j -> s (h j)")[:, hp * 128:(hp + 1) * 128], XY[0:16, 1536:1664])
                else:
                    pb = pbank()
                    for e in range(2):
                        sl = slice(64 * e, 64 * e + 64)
                        kb.mm(pb[sl, 0:65], kw[:, sl], vext[:, 2 * hp + e, :])
                    kb.stt(Cp[hp][:, :], Cp[hp][:, :], decsel[:, 0, hp:hp + 1], pb[:, 0:65], ALU.mult, ALU.add)
                    kb.cp(Cpb[hp][:, :], Cp[hp][:, :], e=kb.pool)
                yield

            if sample:
                for hp_ in range(4):
                    yield from il([mlstm_hp(hp_, SETS[0])])
            else:
                yield from il([mlstm_hp(0, SETS[0]), mlstm_hp(1, SETS[1])])
                yield from il([mlstm_hp(2, SETS[0]), mlstm_hp(3, SETS[1])])
            Hbf = Hb[:].rearrange("p h i -> p (h i)")
            ct2 = ctmp[:, 4:8, :].rearrange("p a t -> p (a t)")
            gob = cacc[:, 0:4, :].rearrange("p a t -> p (a t)")
            gzb = cacc[:, 4:8, :].rearrange("p a t -> p (a t)")
            kb.actf(gob, TM[:, 2, :], AF.Sigmoid)
            kb.actf(gzb, TM[:, 3, :], AF.Sigmoid)
            kb.tt(gob, gob, gzb, ALU.mult, e=kb.pool)
            kb.tt(gob, gob, TM[:, 3, :], ALU.mult, e=kb.pool)
            kb.tt(gob, gob, gnw_b, ALU.mult, e=kb.pool)
            yield
            kb.tt(ct2.rearrange("p (h i) -> p h i", i=64), Hb[:], Hb[:], ALU.mult)
            yield
            kb.rsum(st8b[:, :, 1], ct2.rearrange("p (h i) -> p h i", i=64))
            yield
            rstd_from(st8b[:, :, 2], st8b[:, :, 1], 1.0 / 64, epsb[0:128, 0:1])
            yield
            kb.tt(Hb[:], Hb[:], st8b[:, :, 2:3].to_broadcast([128, 8, 64]), ALU.mult)
            yield
            kb.tt(OBb[:], Hbf, gob, ALU.mult)

        def g_final():
            if (not sample) and n == last_prompt:
                Sf = OA[:].rearrange("p (h j) -> p h j", j=64)
                heads_T(lambda hp_, sl: Hst[hp_][sl, :], Sf[:, 0:4, :])
                yield
                heads_T(lambda hp_, sl: Cp[hp_][sl, 0:64], Sf[:, 4:8, :])
                yield
                for e in range(2):
                    kb.dma(pS_o.rearrange("(hp e) i j -> e i hp j", e=2)[e], Sf[64 * e:64 * e + 64, 0:4, :])
                    kb.dma(pC_o.rearrange("(hp e) i j -> e i hp j", e=2)[e], Sf[64 * e:64 * e + 64, 4:8, :])
                    for hp_ in range(4):
                        kb.dma(pn_o[2 * hp_ + e:2 * hp_ + e + 1, :].rearrange("o j -> j o"), Cp[hp_][64 * e:64 * e + 64, 64:65],
                               allow_slow_non_contiguous=True)
            yield

        def g_tail_a():
            if sample:
                kb.memset(SETS[0].rw["RqTm"][:], 0.0, e=kb.pool)
            for half in range(2):
                pb = pbank()
                for k4 in range(4):
                    k = half * 4 + k4
                    kb.mm(pb[:, k4 * 128:(k4 + 1) * 128], (OAb if half == 0 else OBb)[:, k4 * 128:(k4 + 1) * 128], identb[:])
                kb.acp(oT[:, half * 4:half * 4 + 4, :].rearrange("p k t -> p (k t)"), pb[:, :])
                yield

        def g_tail_b():
            for gi in range(2):
                for half in range(2):
                    wp = epiece(gi * 2 + half).rearrange("p (k n) -> p k n", n=512)
                    pb = pbank()
                    for k in range(8):
                        kb.mm(pb[:, :], hT[:, k, :], wp[:, k, :], start=(k == 0), stop=(k == 7))
                    kb.actf((SGa if gi == 0 else xo)[:, half * 512:(half + 1) * 512], pb[:, :], AF.Sigmoid)
                    yield
            M1 = yt
            for gi in range(2):
                wp = epiece(4 + gi).rearrange("p (k n) -> p k n", n=1024)
                for half in range(2):
                    pb = pbank()
                    for kk_ in range(4):
                        kb.mm(pb[:, :], oT[:, gi * 4 + kk_, :], wp[:, kk_, half * 512:(half + 1) * 512],
                              start=(kk_ == 0), stop=(kk_ == 3))
                    hs = slice(half * 512, (half + 1) * 512)
                    if gi == 0:
                        kb.tt(M1[:, hs], SGa[:, hs], pb[:, :], ALU.mult)
                    else:
                        kb.tt(xo[:, hs], xo[:, hs], pb[:, :], ALU.mult)
                        kb.tt(MGb[:, hs], xo[:, hs], M1[:, hs], ALU.add)
                yield
            for half in range(2):
                pb = pbank()
                for k4 in range(4):
                    k = half * 4 + k4
                    kb.mm(pb[:, k4 * 128:(k4 + 1) * 128], MGb[:, k * 128:(k + 1) * 128], identb[:])
                kb.acp(mgT[:, half * 4:half * 4 + 4, :].rearrange("p k t -> p (k t)"), pb[:, :])
            gt_ = gate_s if sample else gate_p
            for half in range(2):
                wp = epiece(6 + half).rearrange("p (k n) -> p k n", n=512)
                pb = pbank()
                for k in range(8):
                    kb.mm(pb[:, :], mgT[:, k, :], wp[:, k, :], start=(k == 0), stop=(k == 7))
                hs = slice(half * 512, (half + 1) * 512)
                kb.tt(xo[:, hs], pb[:, :], gt_[:, hs], ALU.mult)
                yield
            kb.tt(xo[:], xo[:], x_t[:], ALU.add)
            kb.actf(yt[:], xo[:], AF.Square, accum=ssq[:, 2:3])
            rstd_from(ssq[:, 3:4], ssq[:, 2:3], 1.0 / D, epsb[:, 0:1])
            kb.stt(yt[:], xo[:], ssq[:, 3:4], gfin_b, ALU.mult, ALU.mult)
            kb.dma(y_o[n], yt[:])
            yield

        return dict(head=g_head(), rw=g_rwkv(), ml=g_mlstm(), tail_a=g_tail_a(), tail_b=g_tail_b(), final=g_final(), stage=stage)

    order = ([16] + list(range(16))) if order is None else order
    tiles = []
    if order:
        kb.dma(xt[0][:], xall[order[0]])
    for idx, n in enumerate(order):
        tiles.append(tile_body(n, n == 16, idx % 2))
    prev_tail = None
    for idx, n in enumerate(order):
        T_ = tiles[idx]

        def rwfull(T_=T_):
            yield from T_["head"]
            yield from T_["rw"]

        def tail_then_prefetch(tg=prev_tail, idx=idx):
            if tg is not None:
                yield from tg
            if idx + 1 < len(order):
                kb.dma(xt[(idx + 1) % 2][:], xall[order[idx + 1]])
            yield
        if n == 16:
            run_il([tail_then_prefetch()])
            run_il([rwfull()])
            run_il([T_["ml"]])
        else:
            run_il_w([(rwfull(), RW_W), (T_["ml"], 1), (tail_then_prefetch(), 1)])
        run_il([T_["tail_a"]])
        prev_tail = T_["tail_b"]
    if prev_tail is not None:
        run_il([prev_tail, tiles[-1]["final"]])
    kb.finish()
    _LAST_SEQ[0] = list(ring["seq"])
    return nc


_NC_CACHE = {}


def kernel(x_prompt, x_sample, c_prompt, c_sample, state_rwkv_shift, state_rwkv_S, state_mlstm_conv,
           state_mlstm_C, state_mlstm_n, state_mlstm_m, g_norm, w_ada, b_ada, w_in, mu_shift, w_decay2, w0,
           w_iclr2, a0, k_k, k_a, r_k, ln_w, ln_b, conv_w, conv_b, b_i, b_f, gn_w, w_up_a, w_up_b, w_out, g_final):
    f = lambda a: np.ascontiguousarray(np.asarray(a, dtype=np.float32))
    if "nc" not in _NC_CACHE:
        _NC_CACHE["nc"] = build_two_pass()
    nc = _NC_CACHE["nc"]
    fm = lambda v, nchunk: f(v).reshape(nchunk, 128).T
    b_ada_ = f(b_ada)[0]
    ka = f(k_a)[0]
    pfeat = np.zeros((128, 64), np.float32)
    pfeat[:, 0:8] = fm(g_norm[0], 8)
    pfeat[:, 8:16] = fm(b_ada_[0:1024], 8)
    pfeat[:, 16:24] = fm(b_ada_[1024:2048], 8)
    pfeat[:, 24:37] = fm(mu_shift[0], 13)
    pfeat[:, 37:41] = fm(w0[0], 4)
    pfeat[:, 41:45] = fm(a0[0], 4)
    pfeat[:, 45:49] = fm(k_k[0], 4)
    pfeat[:, 49:53] = fm(ka, 4)
    pfeat[:, 53:57] = fm(r_k[0], 4)
    pconv = np.zeros((128, 40), np.float32)
    cw = f(conv_w)[0]
    for j in range(4):
        pconv[:, j * 8:(j + 1) * 8] = fm(cw[j], 8)
    pconv[:, 32:40] = fm(conv_b[0], 8)
    prow = np.zeros((1, 6 * 1024), np.float32)
    prow[0, 0:1024] = b_ada_[2048:3072]
    prow[0, 1024:2048] = f(g_final)
    prow[0, 2048:2560] = f(ln_w)[0]
    prow[0, 2560:3072] = f(ln_b)[0]
    prow[0, 3072:3584] = f(gn_w)[0]
    pgate = np.zeros((8, 4), np.float32)
    pgate[:, 0] = f(b_i)[0]
    pgate[:, 1] = f(b_f)[0]
    wlora = np.concatenate([f(w_decay2)[0], f(w_iclr2)[0]], axis=0)
    w_in_ = f(w_in)[0]
    wua, wub, wo = f(w_up_a)[0], f(w_up_b)[0], f(w_out)[0]
    epw = np.zeros((8, 128, 4096), np.float32)
    kpn = lambda w_, nk: w_.reshape(nk, 128, -1).transpose(1, 0, 2).reshape(128, -1)
    for gi, c0 in enumerate((C_GLA, C_GLB)):
        for half in range(2):
            epw[gi * 2 + half] = kpn(w_in_[:, c0 + half * 512:c0 + (half + 1) * 512], 8)
    epw[4] = kpn(wua, 4)
    epw[5] = kpn(wub, 4)
    for half in range(2):
        epw[6 + half] = kpn(wo[:, half * 512:(half + 1) * 512], 8)
    xp, xs = f(x_prompt), f(x_sample)
    shared = {"w_ada": f(w_ada)[0], "w_in": w_in_, "epw": epw, "wlora": wlora, "pfeat": pfeat, "prow": prow,
              "pgate": pgate, "pconv": pconv}
    for k, v in _CONSTS.items():
        shared["c_" + k] = v
    in_maps = []
    for c in range(NCORES):
        sl = slice(16 * c, 16 * c + 16)
        m = dict(shared)
        m["xall"] = np.concatenate([xp[c].reshape(16, 128, D), xs[sl].reshape(1, 128, D)], axis=0)
        m["cvec"] = np.concatenate([f(c_prompt)[c:c + 1], f(c_sample)[sl]], axis=0)
        m["shift0"] = f(state_rwkv_shift)[0, sl]
        m["S0"] = f(state_rwkv_S)[0, sl]
        m["conv0"] = f(state_mlstm_conv)[0, sl].reshape(48, 1024)
        m["C0"] = f(state_mlstm_C)[0, sl]
        m["n0"] = f(state_mlstm_n)[0, sl]
        m["m0"] = f(state_mlstm_m)[0, sl]
        in_maps.append(m)
    res = run_bass_kernel_spmd(nc, in_maps, core_ids=list(range(NCORES)))
    R = res.results
    cat = lambda k: np.stack([np.asarray(R[c][k]) for c in range(NCORES)])
    y = cat("y")
    y_prompt = y[:, 0:16].reshape(8, 2048, D)
    y_sample = y[:, 16].reshape(128, 8, D)
    outs = (y_prompt, y_sample,
            cat("p_shift")[None], cat("p_S")[None], cat("p_conv")[None], cat("p_C")[None], cat("p_n")[None],
            cat("p_m").reshape(8, 8)[None],
            cat("s_shift").reshape(128, SHIFT_W)[None], cat("s_S").reshape(128, 8, 64, 64)[None],
            cat("s_conv").reshape(128, 3, 1024)[None], cat("s_C").reshape(128, 8, 64, 64)[None],
            cat("s_n").reshape(128, 8, 64)[None], cat("s_m").reshape(128, 8)[None])
    return tuple(np.ascontiguousarray(o, dtype=np.float32) for o in outs)
```

```python
import numpy as np
import concourse.bass as bass
import concourse.mybir as mybir
from concourse.bass_utils import run_bass_kernel_spmd

F32 = mybir.dt.float32
BF16 = mybir.dt.bfloat16
ALU = mybir.AluOpType
AF = mybir.ActivationFunctionType
AX = mybir.AxisListType

NCORES = 8
D = 1024
NT = 17
SHIFT_W = 1664
NRES = 4752
C_ZA, C_QK, C_VB, C_OB, C_I, C_F, C_ZB, C_GLA, C_GLB = 1664, 2176, 3200, 3712, 4224, 4232, 4240, 4752, 5776
EXPM05 = float(np.exp(-0.5))
NEG = -30000.0
RW_W = 2
MX = BF16


class Src:
    def __init__(self, sem, scale):
        self.sem, self.scale, self.count = sem, scale, 0


class Eng:
    def __init__(self, name, h, src, inorder_safe=False):
        self.name, self.h, self.src, self.safe = name, h, src, inorder_safe
        self.clock = {}


class KB:
    def __init__(self, nc):
        self.nc = nc
        mk = lambda n: Src(nc.alloc_semaphore(n), 1)
        self.pe = Eng("pe", nc.tensor, mk("s_pe"), True)
        self.act = Eng("act", nc.scalar, mk("s_act"))
        self.dve = Eng("dve", nc.vector, mk("s_dve"))
        self.pool = Eng("pool", nc.gpsimd, mk("s_pool"))
        self.sp = Eng("sp", nc.sync, mk("s_sp"))
        self.engs = [self.pe, self.act, self.dve, self.pool, self.sp]
        self.dslots = [Src(nc.alloc_semaphore(f"s_dma{i}"), 16) for i in range(24)]
        self.dnext = 0
        self.swslots = []
        self.lastw = {}
        self.readers = {}

    def _need(self, eng, src, seq):
        if src is eng.src and eng.safe:
            return
        if eng.clock.get(id(src), 0) < seq:
            eng.h.wait_ge(src.sem, seq * src.scale)
            eng.clock[id(src)] = seq

    def _deps(self, eng, ins, outs):
        for r in ins:
            if r in self.lastw:
                self._need(eng, *self.lastw[r])
        for r in outs:
            if r in self.lastw:
                self._need(eng, *self.lastw[r])
            for (s, q) in self.readers.get(r, {}).values():
                self._need(eng, s, q)

    def _commit(self, src, ins, outs):
        src.count += 1
        seq = src.count
        for r in ins:
            self.readers.setdefault(r, {})[id(src)] = (src, seq)
        for r in outs:
            self.lastw[r] = (src, seq)
            self.readers[r] = {}
        return seq

    @staticmethod
    def _names(aps):
        return [a.name for a in aps if a is not None and hasattr(a, "name")]

    def op(self, eng, fn, outs, ins):
        o, i = self._names(outs), self._names(ins)
        self._deps(eng, i, o)
        fn().then_inc(eng.src.sem, 1)
        self._commit(eng.src, i, o)

    def dma(self, out, in_, eng=None, **kw):
        eng = eng or self.sp
        o, i = self._names([out]), self._names([in_])
        if eng is self.pool:
            slot = Src(self.nc.alloc_semaphore(f"s_sw{len(self.swslots)}"), 16)
            self.swslots.append(slot)
        else:
            slot = self.dslots[self.dnext]
            self.dnext = (self.dnext + 1) % len(self.dslots)
        self._deps(eng, i, o)
        if slot.count:
            self._need(eng, slot, slot.count)
        eng.h.dma_start(out=out, in_=in_, **kw).then_inc(slot.sem, 16)
        self._commit(slot, i, o)

    def finish(self):
        for slot in self.dslots + self.swslots:
            if slot.count:
                self._need(self.sp, slot, slot.count)
        for e in self.engs:
            for e2 in self.engs:
                if e2 is not e and e2.src.count:
                    self._need(e, e2.src, e2.src.count)

    def mm(self, out, lhsT, rhs, start=True, stop=True):
        self.op(self.pe, lambda: self.nc.tensor.matmul(out, lhsT=lhsT, rhs=rhs, start=start, stop=stop,
                                                       skip_group_check=True), [out], [lhsT, rhs])

    def actf(self, out, in_, func, bias=None, scale=1.0, accum=None):
        kw = {}
        if bias is not None:
            kw["bias"] = bias
        if accum is not None:
            kw["accum_out"] = accum
        ins = [in_] + ([bias] if hasattr(bias, "name") else []) + ([scale] if hasattr(scale, "name") else [])
        self.op(self.act, lambda: self.nc.scalar.activation(out=out, in_=in_, func=func, scale=scale, **kw),
                [out] + ([accum] if accum is not None else []), ins)

    def _v(self, e):
        return self.dve if e is None else e

    def tt(self, out, a, b, op, e=None):
        e = self._v(e)
        self.op(e, lambda: e.h.tensor_tensor(out=out, in0=a, in1=b, op=op), [out], [a, b])

    def ts(self, out, a, s1, s2=None, op0=ALU.mult, op1=None, e=None):
        e = self._v(e)
        ins = [a] + [s for s in (s1, s2) if hasattr(s, "name")]
        if op1 is None:
            self.op(e, lambda: e.h.tensor_scalar(out=out, in0=a, scalar1=s1, scalar2=None, op0=op0), [out], ins)
        else:
            self.op(e, lambda: e.h.tensor_scalar(out=out, in0=a, scalar1=s1, scalar2=s2, op0=op0, op1=op1),
                    [out], ins)

    def stt(self, out, a, s, b, op0, op1, e=None):
        e = self._v(e)
        ins = [a, b] + ([s] if hasattr(s, "name") else [])
        self.op(e, lambda: e.h.scalar_tensor_tensor(out=out, in0=a, scalar=s, in1=b, op0=op0, op1=op1), [out], ins)

    def cp(self, out, in_, e=None):
        e = self._v(e)
        self.op(e, lambda: e.h.tensor_copy(out=out, in_=in_), [out], [in_])

    def acp(self, out, in_):
        self.actf(out, in_, AF.Copy)

    def memset(self, out, val, e=None):
        e = self._v(e)
        self.op(e, lambda: e.h.memset(out, val), [out], [])

    def scan(self, out, d0, d1, init, op0, op1):
        ins = [d0, d1] + ([init] if hasattr(init, "name") else [])
        self.op(self.dve, lambda: self.nc.vector.tensor_tensor_scan(out=out, data0=d0, data1=d1, initial=init,
                                                                    op0=op0, op1=op1), [out], ins)

    def rsum(self, out, in_):
        self.op(self.dve, lambda: self.nc.vector.tensor_reduce(out=out, in_=in_, axis=AX.X, op=ALU.add), [out], [in_])

    def recip(self, out, in_):
        self.op(self.dve, lambda: self.nc.vector.reciprocal(out=out, in_=in_), [out], [in_])


def host_consts():
    c = {}
    t = np.arange(128)
    ident = np.eye(128, dtype=np.float32)
    c["ident"] = ident

    def masks(L):
        same = (t[:, None] // L) == (t[None, :] // L)
        strict = (same & (t[:, None] < t[None, :])).astype(np.float32)
        incl = (same & (t[:, None] <= t[None, :])).astype(np.float32)
        mA = np.concatenate([strict, incl], axis=1)
        mB = np.concatenate([strict, -incl], axis=1)
        mN = strict.T.copy()
        mbias = np.where(incl > 0, 0.0, NEG).astype(np.float32)
        nc_ = 128 // L
        rowm = (t[:, None] // L == np.arange(nc_)[None, :]).astype(np.float32)
        rst = np.broadcast_to((t % L != 0).astype(np.float32)[None], (128, 128)).copy()
        return mA, mB, mN, mbias, rowm, rst

    for nm, L in (("p", 64), ("s", 8)):
        mA, mB, mN, mbias, rowm, rst = masks(L)
        c["mA" + nm], c["mB" + nm], c["mN" + nm], c["rst" + nm] = mA, mB, mN, rst
        c["rowm" + nm] = rowm
    c["mbp"] = masks(128)[3]
    c["mbs"] = masks(8)[3]
    bo = np.zeros((128, 128), np.float32); bo[:64, :64] = 1; bo[64:, 64:] = 1
    c["blockones"] = bo
    bs = np.zeros((128, 2), np.float32); bs[:64, 0] = 1; bs[64:, 1] = 1
    c["blocksel"] = bs
    selh = np.zeros((8, 8, 128), np.float32)
    for h in range(8):
        selh[h, h, :] = 1
    c["selh"] = selh
    c["ones8"] = np.ones((8, 128), np.float32)
    c["i8"] = np.eye(8, dtype=np.float32)
    r8 = np.zeros((8, 128), np.float32); r8[:, t % 8 != 0] = 1
    c["rst8m"] = r8
    c["rst8a"] = np.where(r8 > 0, 0.0, -1e30).astype(np.float32)
    c["ones8t"] = np.ones((8, 128), np.float32)
    c["zeros8t"] = np.zeros((8, 128), np.float32)
    return c


_CONSTS = host_consts()


_LAST_SEQ = [None]


def build_two_pass(order=None, last_prompt=15):
    build_program(order, last_prompt, seq=None)
    return build_program(order, last_prompt, seq=_LAST_SEQ[0])


def build_program(order=None, last_prompt=15, seq=None):
    nc = bass.Bass("TRN2", target_bir_lowering=False)
    kb = KB(nc)
    di = lambda n, s: nc.dram_tensor(n, list(s), F32, kind="ExternalInput").ap()
    do = lambda n, s: nc.dram_tensor(n, list(s), F32, kind="ExternalOutput").ap()

    xall = di("xall", (NT, 128, D))
    cvec = di("cvec", (17, D))
    shift0 = di("shift0", (16, SHIFT_W))
    S0 = di("S0", (16, 8, 64, 64))
    conv0 = di("conv0", (48, 1024))
    C0 = di("C0", (16, 8, 64, 64))
    n0 = di("n0", (16, 8, 64))
    m0 = di("m0", (16, 8))
    w_ada = di("w_ada", (D, 3072))
    w_in = di("w_in", (D, 6800))
    epw = di("epw", (8, 128, 4096))
    wlora_d = di("wlora", (128, 512))
    pf = di("pfeat", (128, 64))
    prow = di("prow", (1, 6 * 1024))
    pg = di("pgate", (8, 4))
    cd = {k: di("c_" + k, v.shape) for k, v in _CONSTS.items()}

    y_o = do("y", (NT, 128, D))
    pshift_o = do("p_shift", (SHIFT_W,))
    pS_o = do("p_S", (8, 64, 64))
    pconv_o = do("p_conv", (3, 1024))
    pC_o = do("p_C", (8, 64, 64))
    pn_o = do("p_n", (8, 64))
    pm_o = do("p_m", (8, 1))
    sshift_o = do("s_shift", (16, SHIFT_W))
    sS_o = do("s_S", (16, 8, 64, 64))
    sconv_o = do("s_conv", (16, 3, 1024))
    sC_o = do("s_C", (16, 8, 64, 64))
    sn_o = do("s_n", (16, 8, 64))
    sm_o = do("s_m", (16, 8))
    epwb = nc.dram_tensor("epwb", [8, 128, 4096], BF16, kind="Internal").ap()

    sb = lambda n, s, dt=F32: nc.alloc_sbuf_tensor("sb_" + n, list(s), dt)
    pairs = [nc.alloc_psum_tensor(f"pp{i}", [128, 1024], F32) for i in range(4)]
    held = set()
    pi = [0]
    half = [None]

    def _next_pair():
        for _ in range(8):
            i = pi[0] % 4
            pi[0] += 1
            if i not in held:
                return i
        raise RuntimeError("no free PSUM pair")

    def pbank():
        if half[0] is None:
            i = _next_pair()
            half[0] = i
            return pairs[i][:, 0:512]
        i = half[0]
        half[0] = None
        return pairs[i][:, 512:1024]

    def ppair():
        half[0] = None
        return pairs[_next_pair()][:].rearrange("p (b c) -> p b c", b=2)

    def phold():
        half[0] = None
        i = _next_pair()
        held.add(i)
        return i, pairs[i][:].rearrange("p (b c) -> p b c", b=2)

    def prelease(i):
        held.discard(i)

    cs = {}
    for k, v in _CONSTS.items():
        cs[k] = sb("k_" + k, v.shape)
        kb.dma(cs[k][:], cd[k])
    ident = cs["ident"]
    pfeat = sb("pfeat", (128, 64))
    kb.dma(pfeat[:], pf)
    gT, bshT, bscT = pfeat[:, 0:8], pfeat[:, 8:16], pfeat[:, 16:24]
    muT, w0T, a0T, kkT, kaT, rkT, omkaT = (pfeat[:, 24:37], pfeat[:, 37:41], pfeat[:, 41:45], pfeat[:, 45:49],
                                           pfeat[:, 49:53], pfeat[:, 53:57], pfeat[:, 57:61])
    pconv = sb("pconv", (128, 40))
    kb.dma(pconv[:], di("pconv", (128, 40)))
    XY = sb("XY", (128, 2048))
    xo, yt = XY[:, 0:1024], XY[:, 1024:2048]
    rows = sb("rows", (128, 2560))
    kb.dma(rows[:], prow[:, 1024:3584].to_broadcast([128, 2560]))
    kb.dma(XY[:, 0:1024], prow[:, 0:1024].to_broadcast([128, 1024]))
    bgate_b, gfin_b = XY[:, 0:1024], rows[:, 0:1024]
    lnw_b, lnb_b, gnw_b = rows[:, 1024:1536], rows[:, 1536:2048], rows[:, 2048:2560]
    pgate = sb("pgate", (8, 4))
    kb.dma(pgate[:], pg)
    kb.ts(pgate[:, 1:2], pgate[:, 1:2], -1.0, None, ALU.mult)
    kb.ts(pfeat[:, 57:61], pfeat[:, 49:53], -1.0, 1.0, ALU.mult, ALU.add)
    wlora = sb("wlora", (128, 512))
    kb.dma(wlora[:], wlora_d)
    w_inb = nc.dram_tensor("w_inb", [D, NRES], BF16, kind="Internal").ap()
    for k in range(8):
        kb.dma(w_inb[k * 128:(k + 1) * 128, :], w_in[k * 128:(k + 1) * 128, 0:NRES], eng=kb.pool)
    w_inb_v = w_inb.rearrange("(k p) n -> p k n", p=128)
    stg = [sb(f"stg{i}", (128, 4096), BF16) for i in range(3)]
    sti = [0]

    TILE_PIECES = ([(c, 512) for c in (0, 512, 1024)] + [(1536, 128), (C_ZA, 512)] +
                   [(C_VB, 512), (C_OB, 512), (C_ZB, 512), (C_QK, 512), (C_QK + 512, 512), (C_I, 16)] +
                   [("e", m) for m in range(8)])
    ring = {"seq": list(seq) if seq is not None else [], "issued": 0, "used": 0, "rec": seq is None}

    def _pview(j):
        spec = ring["seq"][j]
        st = stg[j % 3]
        if spec[0] == "e":
            return st[:], epwb[spec[1]]
        col0, ncols = spec
        return (st[:, 0:8 * ncols].rearrange("p (k n) -> p k n", n=ncols), w_inb_v[:, :, col0:col0 + ncols])

    def next_piece(spec):
        j = ring["used"]
        if ring["rec"]:
            ring["seq"].append(spec)
        assert ring["seq"][j] == spec, (j, ring["seq"][j], spec)
        while ring["issued"] < min(len(ring["seq"]), j + 3):
            v, src = _pview(ring["issued"])
            kb.dma(v, src)
            ring["issued"] += 1
        ring["used"] += 1
        return _pview(j)[0]

    def wpiece(col0, ncols):
        return next_piece((col0, ncols))

    def epiece(m):
        return next_piece(("e", m))

    QK = sb("QK", (128, 8, 128), MX)
    TM = sb("TM", (128, 4, 512))
    PSb = sb("PSb", (128, 13, 128))
    PSs = sb("PSs", (128, 13, 128))
    hTf = sb("hTf", (128, 8, 128))
    crow = XY[0:17, 1024:2048]
    kb.dma(crow, cvec)
    csig = sb("csig", (17, D))
    kb.actf(csig[:], crow[:], AF.Sigmoid)
    kb.tt(csig[:], csig[:], crow[:], ALU.mult)
    sT = sb("sT", (128, 8, 17))
    pb = pbank()
    for k in range(8):
        kb.mm(pb[:, k * 17:(k + 1) * 17], csig[:, k * 128:(k + 1) * 128], ident[0:17, 0:17])
    kb.cp(sT[:].rearrange("p k s -> p (k s)"), pb[:, 0:136])
    sTp = TM[:, 0:2, :].rearrange("p a (k t) -> p (a k) t", t=128)
    sTs = PSb[:, 0:8, :]
    kb.cp(sTp, sT[:, :, 0:1].to_broadcast([128, 8, 128]))
    kb.cp(sTs.rearrange("p k (s t) -> p k s t", t=8), sT[:, :, 1:17].unsqueeze(3).to_broadcast([128, 8, 16, 8]))
    A_T = sb("A_T", (128, 8, 17))
    B_T = sb("B_T", (128, 8, 17))
    gate_p = sb("gate_p", (128, D))
    gate_s = sb("gate_s", (128, D))
    wst = [hTf, PSs[:, 0:8, :]]
    w_ada_v = w_ada.rearrange("(k p) n -> p k n", p=128)
    hA, ppA = phold()
    pbA, pbB = ppA[:, 0, :], ppA[:, 1, :]
    for blk in range(24):
        ws_ = wst[blk % 2]
        kb.dma(ws_[:] if blk % 2 == 0 else ws_, w_ada_v[:, :, blk * 128:(blk + 1) * 128])
        wv = (lambda k: ws_[:, k, :])
        if blk < 16:
            tg = pbA if blk < 8 else pbB
            fc = blk % 8
            for k in range(8):
                kb.mm(tg[:, fc * 17:(fc + 1) * 17], wv(k), sT[:, k, :], start=(k == 0), stop=(k == 7))
        else:
            n0_ = (blk - 16) * 128
            for (lt, dstg) in ((sTp, gate_p), (sTs, gate_s)):
                pb = pbank()
                for k in range(8):
                    kb.mm(pb[:, 0:128], lt[:, k, :], wv(k), start=(k == 0), stop=(k == 7))
                kb.tt(dstg[:, n0_:n0_ + 128], pb[:, 0:128], bgate_b[:, n0_:n0_ + 128], ALU.add)
    kb.cp(B_T[:].rearrange("p a s -> p (a s)"), pbA[:, 0:136])
    kb.cp(A_T[:].rearrange("p a s -> p (a s)"), pbB[:, 0:136])
    prelease(hA)
    kb.tt(B_T[:], B_T[:], bshT.unsqueeze(2).to_broadcast([128, 8, 17]), ALU.add)
    kb.tt(A_T[:], A_T[:], bscT.unsqueeze(2).to_broadcast([128, 8, 17]), ALU.add)
    kb.ts(A_T[:], A_T[:], 1.0, None, ALU.add)
    kb.tt(A_T[:], A_T[:], gT.unsqueeze(2).to_broadcast([128, 8, 17]), ALU.mult)
    dly = sb("dly", (8, 1))
    kb.cp(dly[:], A_T[0:8, 0, 0:1], e=kb.pool)
    for m in range(8):
        kb.dma(epwb[m], epw[m], eng=kb.pool)

    Hst = [sb(f"Hst{i}", (128, 64)) for i in range(4)]
    Cp = [sb(f"Cp{i}", (128, 65)) for i in range(4)]
    for i in range(4):
        kb.memset(Hst[i][:], 0.0)
        kb.memset(Cp[i][:], 0.0)
    carry_ps = sb("carry_ps", (128, 13))
    carry_qk = sb("carry_qk", (128, 8, 3))
    kb.memset(carry_ps[:], 0.0)
    kb.memset(carry_qk[:], 0.0)
    Bn_c = sb("Bn_c", (8, 1))
    M_c = sb("M_c", (8, 1))
    kb.memset(Bn_c[:], 0.0)
    kb.memset(M_c[:], 0.0)
    epsb = sb("epsb", (128, 4))
    kb.memset(epsb[:, 0:1], 1e-6)
    kb.memset(epsb[:, 1:2], 64e-5)
    kb.memset(epsb[:, 2:3], 1.0)
    kb.memset(epsb[:, 3:4], 1e-24)

    xt = [sb(f"xt{i}", (128, D)) for i in range(2)]
    ssqs = [sb(f"ssq{i}", (128, 4)) for i in range(2)]
    diags = [sb(f"diag{i}", (128, 128)) for i in range(2)]
    hTs = [sb(f"hT{i}", (128, 8, 128), BF16) for i in range(2)]
    QKb = sb("QKb", (128, 8, 176))
    OA = sb("OA", (128, 512))
    OB = sb("OB", (128, 512))
    OAb = sb("OAb", (128, 512), BF16)
    OBb = sb("OBb", (128, 512), BF16)
    class BSet:
        pass

    def mkset(t, big):
        S = BSet()
        f = {n: sb(f"f{t}_" + n, (128, 128)) for n in
             ("sg", "a", "logp", "kk", "t1", "ep", "em", "epp", "pls", "RqT", "rkr")}
        f["KN"] = sb(f"f{t}_KN", (128, 2, 128))
        f["KH"] = sb(f"f{t}_KH", (128, 2, 128), MX)
        f["KS"] = sb(f"f{t}_KS", (128, 2, 128), MX)
        nc_max = 16 if big else 2
        S.rw = dict(f=f, X2=sb(f"X2{t}", (128, 2, 128), MX), pLt=sb(f"pLt{t}", (128, 16)),
                    ATk=sb(f"ATk{t}", (128, 2, 256), MX), ATb=sb(f"ATb{t}", (128, 2, 256), MX),
                    NMa=sb(f"NMa{t}", (128, 2, 2, 128), MX), NMb=sb(f"NMb{t}", (128, 2, 2, 128), MX),
                    Xa=sb(f"Xa{t}", (128, 256), MX), Xb=sb(f"Xb{t}", (128, 256), MX),
                    KBt=sb(f"KBt{t}", (128, 2, 128), MX), Yi=sb(f"Yi{t}", (128, 128)),
                    mk4=[sb(f"mk{t}_{i}", (128, 2 * (4 if big else 2) * 65), MX) for i in range(4)],
                    RqTm=sb(f"RqTm{t}", (128, (4 if big else 2) * 128)),
                    PhiT=sb(f"PhiT{t}", (128, nc_max * 64)), Psi=sb(f"Psi{t}", (128, nc_max * 64)),
                    tmpH=sb(f"tmpH{t}", (128, 65)))
        S.ml = dict(ktok=sb(f"ktok{t}", (128, 128), MX), kw=sb(f"kw{t}", (128, 128), MX),
                    Wt=sb(f"Wt{t}", (128, 2, 128)), PT=sb(f"PT{t}", (128, 2, 128), MX),
                    hnd=sb(f"hnd{t}", (128, 2, 65)), st16=sb(f"st16{t}", (128, 16)))
        return S
    SETS = [mkset(0, True), mkset(1, False)]
    for S_ in SETS:
        kb.memset(S_.rw["RqTm"][:], 0.0)
    twb = sb("f_tw", (128, 128))
    identb = sb("identb", (128, 128), MX)
    kb.cp(identb[:], ident[:])
    Vt = sb("Vt", (128, 512), MX)
    Yall = OA
    bon = sb("bon", (128, 8))
    Hs_in = sb("Hs_in", (128, 16, 64))
    H0s = sb("H0s", (128, 16, 65))
    Hn = sb("Hn", (128, 16, 65))
    So = Hs_in
    st8 = sb("st8", (128, 8, 4))
    st8b = sb("st8b", (128, 8, 4))
    g8 = {n: sb("g_" + n, (8, 128)) for n in ("sp", "Bn", "a", "M", "t")}
    G4 = sb("G4", (8, 4, 128))
    negM = sb("negM", (8, 128))
    m0T = sb("m0T", (8, 16))
    dec8 = sb("dec8", (8, 16))
    ddec = sb("ddec", (8, 16, 8))
    decsel = sb("decsel", (128, 16, 4))
    mT8 = sb("mT8", (8, 16))
    gtok = sb("gtok", (128, 4, 8))
    cacc = H0s[:].rearrange("p s c -> p (s c)")[:, 0:1024].rearrange("p (k t) -> p k t", t=128)
    ctmp = Hn[:].rearrange("p s c -> p (s c)")[:, 0:1024].rearrange("p (k t) -> p k t", t=128)
    rtmp = PSs[:, 0:4, :]
    vext = sb("vext", (128, 8, 65), MX)
    Cpb = [sb(f"Cpb{i}", (128, 65), MX) for i in range(4)]
    for i in range(4):
        kb.memset(Cpb[i][:], 0.0)
    Hb = OB[:].rearrange("p (h i) -> p h i", i=64)
    oTs = [sb("oT", (128, 8, 128), BF16),
           gate_s[:].bitcast(BF16)[:, 0:1024].rearrange("p (k t) -> p k t", t=128)]
    SGa = Hs_in[:].rearrange("p s j -> p (s j)")
    MGb = sb("MGb", (128, 1024), BF16)
    mgT = sb("mgT", (128, 8, 128), BF16)
    sh0r = XY[0:16, 0:SHIFT_W]
    cv0r = XY[0:48, 0:1024]

    kb.memset(vext[:, :, 64:65], 1.0)

    def rstd_from(out, in_, scale, eps_ap):
        kb.actf(out, in_, AF.Ln, bias=eps_ap, scale=scale)
        kb.actf(out, out, AF.Exp, scale=-0.5)

    def il(gens):
        gens = list(gens)
        while gens:
            for g_ in list(gens):
                try:
                    next(g_)
                except StopIteration:
                    gens.remove(g_)
            yield

    def run_il(gens):
        for _ in il(gens):
            pass

    def run_il_w(gens_w):
        gens_w = list(gens_w)
        while gens_w:
            for item in list(gens_w):
                g_, w_ = item
                for _ in range(w_):
                    try:
                        next(g_)
                    except StopIteration:
                        gens_w.remove(item)
                        break

    def until_pre(g_, st):
        while not st["rw_pre"]:
            try:
                next(g_)
            except StopIteration:
                return
            yield

    def tile_body(n, sample, xs):
        L = 8 if sample else 64
        NC = 128 // L
        nm = "s" if sample else "p"
        mA, mB, mN, rstm = cs["mA" + nm], cs["mB" + nm], cs["mN" + nm], cs["rst" + nm]
        rowm = cs["rowm" + nm]
        nlev = 3 if sample else 6
        x_t = xt[xs]
        hT, ssq, diag = hTs[xs], ssqs[xs], diags[xs]
        oT = oTs[xs]
        stage = {"rw_pre": False}

        def rstd_from(out, in_, scale, eps_ap):
            kb.actf(out, in_, AF.Ln, bias=eps_ap, scale=scale)
            kb.actf(out, out, AF.Exp, scale=-0.5)

        def g_head():
            kb.memset(ssq[:], 0.0)
            kb.actf(hTf[:].rearrange("p k t -> p (k t)"), x_t[:], AF.Square, accum=ssq[:, 0:1])
            rstd_from(ssq[:, 1:2], ssq[:, 0:1], 1.0 / D, epsb[:, 0:1])
            kb.ts(diag[:], ident[:], ssq[:, 1:2], None, ALU.mult)
            for half in range(2):
                pb = pbank()
                for k4 in range(4):
                    k = half * 4 + k4
                    kb.mm(pb[:, k4 * 128:(k4 + 1) * 128], x_t[:, k * 128:(k + 1) * 128], diag[:])
                kb.acp(hTf[:, half * 4:half * 4 + 4, :].rearrange("p k t -> p (k t)"), pb[:, :])
                yield
            if sample:
                Ab = A_T[:, :, 1:17].unsqueeze(3).to_broadcast([128, 8, 16, 8])
                Bb = B_T[:, :, 1:17].unsqueeze(3).to_broadcast([128, 8, 16, 8])
                hv = hTf[:].rearrange("p k (s t) -> p k s t", t=8)
                kb.tt(hv, hv, Ab, ALU.mult)
                kb.tt(hT[:].rearrange("p k (s t) -> p k s t", t=8), hv, Bb, ALU.add)
            else:
                kb.tt(hTf[:], hTf[:], A_T[:, :, 0:1].to_broadcast([128, 8, 128]), ALU.mult)
                kb.tt(hT[:], hTf[:], B_T[:, :, 0:1].to_broadcast([128, 8, 128]), ALU.add)
            yield

        def inproj_fm(dst_fn, col0, nchunks, sv=None):
            for c0 in range(0, nchunks, 4):
                n4 = min(4, nchunks - c0)
                wp = wpiece(col0 + c0 * 128, n4 * 128)
                pb = pbank()
                for cc in range(n4):
                    for k in range(8):
                        kb.mm(pb[:, cc * 128:(cc + 1) * 128], wp[:, k, cc * 128:(cc + 1) * 128], hT[:, k, :],
                              start=(k == 0), stop=(k == 7))
                src = pb[:, 0:n4 * 128].rearrange("p (c t) -> p c t", t=128)
                kb.acp(dst_fn(c0, n4), sv(src) if sv else src)
                yield

        def inproj_tm(gi, col):
            wp = wpiece(col, 512)
            pb = pbank()
            for k in range(8):
                kb.mm(pb[:, :], hT[:, k, :], wp[:, k, :], start=(k == 0), stop=(k == 7))
            kb.acp(TM[:, gi, :], pb[:, :])
            yield

        G = min(NC, 4)

        def il(gens):
            gens = list(gens)
            while gens:
                for g_ in list(gens):
                    try:
                        next(g_)
                    except StopIteration:
                        gens.remove(g_)
                yield

        def run_il(gens):
            for _ in il(gens):
                pass

        def per_seq_tiled(lhs, rhs, evac):
            for s0 in (0, 8):
                pp = ppair()
                for s in range(s0, s0 + 8):
                    for e in range(2):
                        sl = slice(64 * e, 64 * e + 64)
                        kb.mm(pp[sl, e, (s - s0) * 64:(s - s0) * 64 + 64], lhs(s, sl), rhs(s, sl))
                for e in range(2):
                    sl = slice(64 * e, 64 * e + 64)
                    evac(sl, s0, pp[sl, e, :].rearrange("p (s i) -> p s i", i=64))

        def heads_T(src_fn, dst):
            pp = ppair()
            for hp_ in range(4):
                for e in range(2):
                    sl = slice(64 * e, 64 * e + 64)
                    kb.mm(pp[sl, e, hp_ * 64:hp_ * 64 + 64], src_fn(hp_, sl), ident[sl, sl])
            for e in range(2):
                sl = slice(64 * e, 64 * e + 64)
                kb.cp(dst[sl, :, :], pp[sl, e, 0:256].rearrange("p (h j) -> p h j", j=64))

        def rq_group(src_ap, g0, RqTm):
            if sample:
                kb.memset(RqTm[:], 0.0, e=kb.pool)
            W_ = RqTm[:].shape[1]
            dst = bass.AP(RqTm, g0 * L, [[W_, 128], [128 + L, G], [1, L]])
            kb.cp(dst, src_ap[:, g0 * L:(g0 + G) * L].rearrange("p (c l) -> p c l", l=L))
            return RqTm[:, 0:G * 128].rearrange("p (c t) -> p c t", t=128)

        def g_rwkv():
            yield from inproj_fm(lambda c0, n4: PSb[:, c0:c0 + n4, :], 0, 13)
            yield from inproj_tm(0, C_ZA)
            if sample:
                sv4 = PSs[:].rearrange("p c (s t) -> p c s t", t=8)
                bv = PSb[:].rearrange("p c (s t) -> p c s t", t=8)
                kb.tt(sv4[:, :, :, 1:8], bv[:, :, :, 0:7], bv[:, :, :, 1:8], ALU.subtract)
                kb.dma(sh0r, shift0)
                pb = pbank()
                pb2 = pbank()
                for c in range(13):
                    tgt = pb if c < 8 else pb2
                    cc = c % 8
                    kb.mm(tgt[:, cc * 16:(cc + 1) * 16], sh0r[:, c * 128:(c + 1) * 128], ident[0:16, 0:16])
                kb.tt(sv4[:, 0:8, :, 0], pb[:, 0:128].rearrange("p (c s) -> p c s", s=16), bv[:, 0:8, :, 0], ALU.subtract)
                kb.tt(sv4[:, 8:13, :, 0], pb2[:, 0:80].rearrange("p (c s) -> p c s", s=16), bv[:, 8:13, :, 0], ALU.subtract)
            else:
                kb.tt(PSs[:, :, 1:128], PSb[:, :, 0:127], PSb[:, :, 1:128], ALU.subtract)
                kb.tt(PSs[:, :, 0], carry_ps[:], PSb[:, :, 0], ALU.subtract)
                kb.cp(carry_ps[:], PSb[:, :, 127], e=kb.pool)
            yield
            kb.tt(PSs[:], PSs[:], muT.unsqueeze(2).to_broadcast([128, 13, 128]), ALU.mult)
            yield
            kb.tt(PSs[:], PSs[:], PSb[:], ALU.add)
            if sample:
                scr = XY[:, 1700:1908].rearrange("p (c s) -> p c s", s=16)
                kb.cp(scr, PSb[:].rearrange("p c (s t) -> p c s t", t=8)[:, :, :, 7])
                for c0 in range(0, 13, 4):
                    n4 = min(4, 13 - c0)
                    pb = pbank()
                    for cc in range(n4):
                        kb.mm(pb[0:16, cc * 128:(cc + 1) * 128], scr[:, c0 + cc, :], ident[:])
                    kb.cp(XY[0:16, c0 * 128:(c0 + n4) * 128], pb[0:16, 0:n4 * 128])
                kb.dma(sshift_o, XY[0:16, 0:SHIFT_W])
            elif n == last_prompt:
                kb.dma(pshift_o.rearrange("(c p) -> p c", p=128), PSb[:, :, 127], allow_slow_non_contiguous=True)
            pb = pbank()
            for hp in range(4):
                kb.mm(pb[:, hp * 128:(hp + 1) * 128], PSs[:, 8 + hp, :], ident[:])
            kb.acp(Vt[:], pb[:, :])
            kb.actf(twb[0:64, :], PSs[0:64, 12, :], AF.Tanh)

            stage["rw_pre"] = True

            def rwkv_hp(hp, S):
                R_ = S.rw
                f, X2, pLt, ATk, ATb, NMa, NMb = R_["f"], R_["X2"], R_["pLt"], R_["ATk"], R_["ATb"], R_["NMa"], R_["NMb"]
                Xa, Xb, KBt, Yi, mk4, RqTm = R_["Xa"], R_["Xb"], R_["KBt"], R_["Yi"], R_["mk4"], R_["RqTm"]
                PhiT, Psi, tmpH = R_["PhiT"], R_["Psi"], R_["tmpH"]
                r_, k_ = PSs[:, hp, :], PSs[:, 4 + hp, :]
                KN = f["KN"]
                kb.ts(f["kk"][:], k_, kkT[:, hp:hp + 1], None, ALU.mult)
                kb.tt(f["epp"][:], f["kk"][:], f["kk"][:], ALU.mult)
                pp = ppair()
                kb.mm(pp[:, 0, 0:128], wlora[0:64, hp * 128:(hp + 1) * 128], twb[0:64, :])
                kb.mm(pp[:, 1, 0:128], wlora[64:128, hp * 128:(hp + 1) * 128], PSs[64:128, 12, :])
                pb = pbank()
                kb.mm(pb[:, 0:128], cs["blockones"][:], f["epp"][:])
                kb.actf(f["sg"][:], pp[:, 0, 0:128], AF.Sigmoid, bias=w0T[:, hp:hp + 1])
                kb.actf(f["a"][:], pp[:, 1, 0:128], AF.Sigmoid, bias=a0T[:, hp:hp + 1])
                kb.actf(f["epp"][:], pb[:, 0:128], AF.Ln, bias=epsb[:, 3:4])
                kb.actf(f["epp"][:], f["epp"][:], AF.Exp, scale=-0.5)
                kb.scan(f["logp"][:], rstm[:], f["sg"][:], 0.0, ALU.mult, ALU.add)
                kb.ts(f["t1"][:], f["a"][:], kaT[:, hp:hp + 1], omkaT[:, hp:hp + 1], ALU.mult, ALU.add)
                kb.tt(KN[:, 0, :], k_, f["t1"][:], ALU.mult)
                kb.tt(f["kk"][:], f["kk"][:], f["epp"][:], ALU.mult)
                kb.stt(KN[:, 1, :], f["kk"][:], -1.0, f["a"][:], ALU.mult, ALU.mult)
                kb.stt(f["rkr"][:], r_, rkT[:, hp:hp + 1], KN[:, 0, :], ALU.mult, ALU.mult)
                pb = pbank()
                kb.mm(pb[:, 0:2], f["rkr"][:], cs["blocksel"][:])
                kb.cp(bon[:, 2 * hp:2 * hp + 2], pb[:, 0:2])
                yield
                lp3 = f["logp"][:].rearrange("p (c l) -> p c l", l=L)
                kb.tt(f["t1"][:], f["logp"][:], f["sg"][:], ALU.subtract)
                kb.tt(f["rkr"][:].rearrange("p (c l) -> p c l", l=L), lp3[:, :, L - 1:L].to_broadcast([128, NC, L]), lp3,
                      ALU.subtract)
                kb.actf(f["ep"][:], f["logp"][:], AF.Exp, scale=-EXPM05)
                kb.actf(f["epp"][:], f["t1"][:], AF.Exp, scale=-EXPM05)
                kb.actf(f["em"][:], f["logp"][:], AF.Exp, scale=EXPM05)
                kb.actf(f["pls"][:], f["rkr"][:], AF.Exp, scale=-EXPM05)
                kb.tt(X2[:, 1, :], r_, f["ep"][:], ALU.mult)
                kb.tt(X2[:, 0, :], f["kk"][:], f["epp"][:], ALU.mult)
                kb.tt(f["KH"][:], KN[:], f["em"][:].unsqueeze(1).to_broadcast([128, 2, 128]), ALU.mult)
                kb.cp(pLt[:, 0:NC], f["ep"][:].rearrange("p (c l) -> p c l", l=L)[:, :, L - 1])
                kb.tt(f["KS"][:], KN[:], f["pls"][:].unsqueeze(1).to_broadcast([128, 2, 128]), ALU.mult)
                yield
                pb = pbank()
                kb.mm(pb[:, 0:128], f["KS"][:, 0, :], identb[:])
                kb.mm(pb[:, 128:256], f["KS"][:, 1, :], identb[:])
                kb.acp(KBt[:].rearrange("p a j -> p (a j)"), pb[:, 0:256])
                if not sample:
                    rm0 = rowm[:, 0:G].unsqueeze(1).unsqueeze(3).to_broadcast([128, 2, G, 64])
                    for i4, s_ in ((0, KBt[:, 1, :].rearrange("p (e j) -> p e j", e=2)),
                                   (2, Vt[:, hp * 128:(hp + 1) * 128].rearrange("p (e j) -> p e j", e=2))):
                        kb.tt(mk4[i4][:, 0:2 * G * 64].rearrange("p (e c j) -> p e c j", e=2, j=64),
                              s_.unsqueeze(2).to_broadcast([128, 2, G, 64]), rm0, ALU.mult, e=kb.pool)
                yield
                pk, pbb = ppair(), ppair()
                X2f = X2[:].rearrange("p a t -> p (a t)")
                for e in range(2):
                    sl = slice(64 * e, 64 * e + 64)
                    kb.mm(pk[:, e, 0:256], f["KH"][sl, 0, :], X2f[sl, :])
                    kb.mm(pbb[:, e, 0:256], f["KH"][sl, 1, :], X2f[sl, :])
                    kb.mm(pk[:, e, 256:384], X2[sl, 0, :], f["KH"][sl, 1, :])
                kb.tt(ATk[:], pk[:, :, 0:256], mA[:].unsqueeze(1).to_broadcast([128, 2, 256]), ALU.mult)
                kb.tt(ATb[:], pbb[:, :, 0:256], mA[:].unsqueeze(1).to_broadcast([128, 2, 256]), ALU.mult)
                kb.tt(NMa[:, 0, :, :], pk[:, :, 256:384], mN[:].unsqueeze(1).to_broadcast([128, 2, 128]), ALU.mult)
                kb.cp(NMa[:, 1, :, :], ATb[:, :, 0:128], e=kb.pool)
                yield
                pp = ppair()
                for e in range(2):
                    kb.mm(pp[:, e, 0:64], ATk[:, e, 0:128], Vt[:, hp * 128 + 64 * e: hp * 128 + 64 * e + 64])
                for e in range(2):
                    sl = slice(64 * e, 64 * e + 64)
                    kb.mm(pp[:, e, 64:128], X2[sl, 0, :], identb[sl, sl])
                kb.cp(Xa[:].rearrange("p (e c) -> p e c", e=2), pp[:, :, 0:128])
                yield
                cur, nxt = NMa, NMb
                xc, xn = Xa, Xb
                for lev in range(nlev):
                    pa = pbank()
                    for e in range(2):
                        kb.mm(pa[:, 128 * e:128 * e + 128], cur[:, 1, e, :], xc[:, 128 * e:128 * e + 128])
                    if lev < nlev - 1:
                        pp = pbank()
                        pp4 = pp[:, :].rearrange("p (a e c) -> p a e c", a=2, e=2)
                        for e in range(2):
                            kb.mm(pp4[:, 0, e, :], cur[:, 1, e, :], cur[:, 0, e, :])
                            kb.mm(pp4[:, 1, e, :], cur[:, 0, e, :], cur[:, 1, e, :])
                        kb.acp(nxt[:].rearrange("p a e c -> p (a e c)"), pp[:, :])
                    kb.tt(xn[:], xc[:], pa[:, 0:256], ALU.add)
                    cur, nxt = nxt, cur
                    xc, xn = xn, xc
                    yield
                Xf = xc
                pb = pbank()
                for e in range(2):
                    kb.mm(pb[64 * e:64 * e + 64, 0:128], Xf[:, 128 * e + 64:128 * e + 128], ATb[:, e, 128:256])
                kb.tt(f["RqT"][:], X2[:, 1, :], pb[:, 0:128], ALU.add)
                rq_p = None if sample else rq_group(f["RqT"], 0, RqTm)
                yield
                pb = pbank()
                for e in range(2):
                    kb.mm(pb[:, 64 * e:64 * e + 64], ATk[:, e, 128:256], Vt[:, hp * 128 + 64 * e: hp * 128 + 64 * e + 64],
                          start=True, stop=False)
                    kb.mm(pb[:, 64 * e:64 * e + 64], ATb[:, e, 128:256], Xf[:, 128 * e:128 * e + 64], start=False, stop=True)
                kb.acp(Yi[:], pb[:, 0:128])
                yield
                srcs = (KBt[:, 1, :].rearrange("p (e j) -> p e j", e=2), KBt[:, 0, :].rearrange("p (e j) -> p e j", e=2),
                        Vt[:, hp * 128:(hp + 1) * 128].rearrange("p (e j) -> p e j", e=2),
                        Xf[:].rearrange("p (e c) -> p e c", e=2)[:, :, 0:64])
                for g0 in range(0, NC, G):
                    rm = rowm[:, g0:g0 + G].unsqueeze(1).unsqueeze(3).to_broadcast([128, 2, G, 64])
                    mv = [mk4[i4][:, 0:2 * G * 64].rearrange("p (e c j) -> p e c j", e=2, j=64) for i4 in range(4)]
                    for i4, s_ in enumerate(srcs):
                        if i4 == 1 or (not sample and i4 != 3):
                            continue
                        kb.tt(mv[i4], s_.unsqueeze(2).to_broadcast([128, 2, G, 64]), rm, ALU.mult,
                              e=(None if i4 == 3 else kb.pool))
                    for e in range(2):
                        sl = slice(64 * e, 64 * e + 64)
                        me = [mk4[i4][:, e * G * 64:(e + 1) * G * 64] for i4 in range(4)]
                        pb = pbank()
                        kb.mm(pb[sl, 0:G * 64], Xf[:, 128 * e + 64:128 * e + 128], me[0])
                        kb.acp(PhiT[sl, g0 * 64:(g0 + G) * 64], pb[sl, 0:G * 64])
                        pb = pbank()
                        kb.mm(pb[sl, 0:G * 64], KBt[:, 0, sl], me[2], start=True, stop=False)
                        kb.mm(pb[sl, 0:G * 64], KBt[:, 1, sl], me[3], start=False, stop=True)
                        kb.cp(Psi[sl, g0 * 64:(g0 + G) * 64], pb[sl, 0:G * 64])
                    yield
                PhiT3 = PhiT[:, 0:NC * 64].rearrange("p (c j) -> p c j", j=64)
                Psi3 = Psi[:, 0:NC * 64].rearrange("p (c j) -> p c j", j=64)

                if sample:
                    for e in range(2):
                        kb.dma(Hs_in[64 * e:64 * e + 64, :, :], S0[:, 2 * hp + e, :, :].rearrange("s i j -> i s j"))
                    per_seq_tiled(lambda s, sl: Hs_in[sl, s, :], lambda s, sl: ident[sl, sl],
                                  lambda sl, s0, v: kb.cp(H0s[sl, s0:s0 + 8, 0:64], v))
                    kb.tt(Hn[:, :, 0:64], H0s[:, :, 0:64], pLt[:, 0:16].unsqueeze(2).to_broadcast([128, 16, 64]), ALU.mult)
                    kb.tt(Hn[:, :, 0:64], Hn[:, :, 0:64], Psi3, ALU.add)
                    per_seq_tiled(lambda s, sl: PhiT3[sl, s, :], lambda s, sl: H0s[sl, s, 0:64],
                                  lambda sl, s0, v: kb.tt(Hn[sl, s0:s0 + 8, 0:64], Hn[sl, s0:s0 + 8, 0:64], v, ALU.add))
                    for g0 in range(0, 16, G):
                        rq = rq_group(f["RqT"], g0, RqTm)
                        pq = ppair()
                        for e in range(2):
                            sl = slice(64 * e, 64 * e + 64)
                            for g_ in range(G):
                                kb.mm(pq[:, e, 0:64], rq[sl, g_, :], H0s[sl, g0 + g_, 0:64], start=(g_ == 0), stop=(g_ == G - 1))
                        ydst = Yall[:, hp * 128:(hp + 1) * 128] if g0 + G == 16 else Yi[:]
                        kb.tt(ydst.rearrange("p (e i) -> p e i", e=2), Yi[:].rearrange("p (e i) -> p e i", e=2), pq[:, :, 0:64], ALU.add)
                    per_seq_tiled(lambda s, sl: Hn[sl, s, 0:64], lambda s, sl: ident[sl, sl],
                                  lambda sl, s0, v: kb.cp(So[sl, s0:s0 + 8, :], v))
                    for e in range(2):
                        kb.dma(sS_o[:, 2 * hp + e, :, :].rearrange("s i j -> i s j"), So[64 * e:64 * e + 64, :, :])
                else:
                    rq = rq_p
                    for c in range(NC):
                        kb.stt(tmpH[:, 0:64], Hst[hp][:, :], pLt[:, c:c + 1], Psi3[:, c, :], ALU.mult, ALU.add)
                        pp, pq = ppair(), ppair()
                        for e in range(2):
                            sl = slice(64 * e, 64 * e + 64)
                            kb.mm(pp[sl, e, 0:64], PhiT3[sl, c, :], Hst[hp][sl, :])
                            kb.mm(pq[:, e, 0:64], rq[sl, c, :], Hst[hp][sl, :])
                        for e in range(2):
                            sl = slice(64 * e, 64 * e + 64)
                            kb.tt(Hst[hp][sl, :], tmpH[sl, 0:64], pp[sl, e, 0:64], ALU.add)
                        ydst = Yall[:, hp * 128:(hp + 1) * 128] if c == NC - 1 else Yi[:]
                        kb.tt(ydst.rearrange("p (e i) -> p e i", e=2), Yi[:].rearrange("p (e i) -> p e i", e=2), pq[:, :, 0:64], ALU.add)
                        yield

            if sample:
                for hp_ in range(4):
                    yield from il([rwkv_hp(hp_, SETS[0])])
            else:
                yield from il([rwkv_hp(0, SETS[0]), rwkv_hp(1, SETS[1])])
                yield from il([rwkv_hp(2, SETS[0]), rwkv_hp(3, SETS[1])])

            Y3 = Yall[:].rearrange("p (h i) -> p h i", i=64)
            sgz = PSs[:, 4:8, :].rearrange("p a t -> p (a t)")
            kb.actf(sgz, TM[:, 0, :], AF.Sigmoid)
            kb.tt(sgz, sgz, TM[:, 0, :], ALU.mult, e=kb.pool)
            g2 = PSs[:, 8:12, :].rearrange("p a t -> p (a t)")
            kb.tt(g2.rearrange("p (h i) -> p h i", i=64), Vt[:].rearrange("p (h i) -> p h i", i=64),
                  bon[:].unsqueeze(2).to_broadcast([128, 8, 64]), ALU.mult, e=kb.pool)
            kb.tt(g2, g2, lnb_b, ALU.add, e=kb.pool)
            kb.tt(g2, g2, sgz, ALU.mult, e=kb.pool)
            kb.tt(sgz, sgz, lnw_b, ALU.mult, e=kb.pool)
            yield
            kb.rsum(st8[:, :, 0], Y3)
            yield
            kb.ts(st8[:, :, 0], st8[:, :, 0], -1.0 / 64, None, ALU.mult)
            yield
            kb.tt(Y3, Y3, st8[:, :, 0:1].to_broadcast([128, 8, 64]), ALU.add)
            yield
            kb.tt(rtmp.rearrange("p a (b i) -> p (a b) i", i=64), Y3, Y3, ALU.mult)
            yield
            kb.rsum(st8[:, :, 1], rtmp.rearrange("p a (b i) -> p (a b) i", i=64))
            yield
            rstd_from(st8[:, :, 2], st8[:, :, 1], 1.0 / 64, epsb[0:128, 1:2])
            yield
            kb.tt(Y3, Y3, st8[:, :, 2:3].to_broadcast([128, 8, 64]), ALU.mult)
            yield
            kb.tt(Yall[:], Yall[:], sgz, ALU.mult)
            yield
            kb.tt(OAb[:], Yall[:], g2, ALU.add)

        def g_mlstm():
            while not stage["rw_pre"]:
                yield
            yield from inproj_tm(1, C_VB)
            yield from inproj_tm(2, C_OB)
            yield from inproj_tm(3, C_ZB)
            if sample:
                qv = QKb[:].rearrange("p c (s u) -> p c s u", u=11)
                kb.dma(cv0r[:], conv0)
                pb = pbank()
                for c in range(8):
                    kb.mm(pb[:, c * 48:(c + 1) * 48], cv0r[:, c * 128:(c + 1) * 128], ident[0:48, 0:48])
                kb.cp(qv[:, :, :, 0:3], pb[:, 0:384].rearrange("p (c s u) -> p c s u", c=8, u=3))
                yield from inproj_fm(lambda c0, n4: qv[:, c0:c0 + n4, :, 3:11], C_QK, 8,
                          sv=lambda a: a.rearrange("p c (s u) -> p c s u", u=8))
                scr = XY[:, 1600:1984].rearrange("p (c u s) -> p c u s", u=3, s=16)
                kb.cp(scr, qv[:, :, :, 8:11].rearrange("p c s u -> p c u s"))
                for u in range(3):
                    pp = ppair()
                    for c in range(8):
                        kb.mm(pp[0:16, c // 4, (c % 4) * 128:(c % 4) * 128 + 128], scr[:, c, u, :], ident[:])
                    kb.cp(XY[0:16, 0:1024].rearrange("p (b c) -> p b c", b=2), pp[0:16, :, :])
                    kb.dma(sconv_o[:, u, :], XY[0:16, 0:1024])
                taps = [qv[:, :, :, j:j + 8] for j in range(4)]
                shp = [128, 8, 16, 8]
                cv = lambda t_: t_[:].rearrange("p c (s u) -> p c s u", u=8)
                wb = lambda j: pconv[:, j * 8:(j + 1) * 8].unsqueeze(2).unsqueeze(3).to_broadcast(shp)
                cbb = pconv[:, 32:40].unsqueeze(2).unsqueeze(3).to_broadcast(shp)
            else:
                kb.cp(QKb[:, :, 0:3], carry_qk[:])
                yield from inproj_fm(lambda c0, n4: QKb[:, c0:c0 + n4, 3:131], C_QK, 8)
                kb.cp(carry_qk[:], QKb[:, :, 128:131], e=kb.pool)
                if n == last_prompt:
                    for c in range(8):
                        kb.dma(pconv_o[:, c * 128:(c + 1) * 128].rearrange("u p -> p u"), QKb[:, c, 128:131],
                               allow_slow_non_contiguous=True)
                taps = [QKb[:, :, j:j + 128] for j in range(4)]
                shp = [128, 8, 128]
                cv = lambda t_: t_[:]
                wb = lambda j: pconv[:, j * 8:(j + 1) * 8].unsqueeze(2).to_broadcast(shp)
                cbb = pconv[:, 32:40].unsqueeze(2).to_broadcast(shp)
            yield
            kb.tt(cv(cacc), taps[0], wb(0), ALU.mult)
            for j in range(1, 4):
                kb.tt(cv(ctmp), taps[j], wb(j), ALU.mult, e=kb.pool)
                kb.tt(cv(cacc), cv(cacc), cv(ctmp), ALU.add)
            yield
            kb.tt(cv(cacc), cv(cacc), cbb, ALU.add)
            yield
            kb.actf(ctmp[:], cacc[:], AF.Sigmoid)
            yield
            kb.tt(QK[:, 0:4, :], cacc[:, 0:4, :], ctmp[:, 0:4, :], ALU.mult)
            yield
            kb.stt(QK[:, 4:8, :], cacc[:, 4:8, :], 0.125, ctmp[:, 4:8, :], ALU.mult, ALU.mult)
            pgi, pgf = pbank(), pbank()
            wpg = wpiece(C_I, 16)
            for k in range(8):
                kb.mm(pgi[0:8, 0:128], wpg[:, k, 0:8], hT[:, k, :], start=(k == 0), stop=(k == 7))
            for k in range(8):
                kb.mm(pgf[0:8, 0:128], wpg[:, k, 8:16], hT[:, k, :], start=(k == 0), stop=(k == 7))
            g = g8
            kb.actf(g["sp"][:], pgf[0:8, 0:128], AF.Exp, bias=pgate[:, 1:2], scale=-1.0)
            kb.actf(g["sp"][:], g["sp"][:], AF.Ln, bias=epsb[0:8, 2:3])
            NCm = 16 if sample else 1
            Lm = 128 // NCm
            if sample:
                kb.dma(m0T[:], m0.rearrange("s h -> h s"), allow_slow_non_contiguous=True)
                kb.scan(g["Bn"][:], cs["rst8m"][:], g["sp"][:], 0.0, ALU.mult, ALU.add)
            else:
                kb.scan(g["Bn"][:], cs["ones8t"][:], g["sp"][:], Bn_c[:, 0:1], ALU.mult, ALU.add)
            kb.stt(g["a"][:], pgi[0:8, 0:128], pgate[:, 0:1], g["Bn"][:], ALU.add, ALU.add)
            if sample:
                a3 = g["a"][:].rearrange("h (s u) -> h s u", u=8)
                kb.cp(g["t"][:], g["a"][:])
                t3 = g["t"][:].rearrange("h (s u) -> h s u", u=8)
                kb.tt(t3[:, :, 0], a3[:, :, 0], m0T[:], ALU.max)
                kb.scan(g["M"][:], cs["rst8a"][:], g["t"][:], 0.0, ALU.add, ALU.max)
                M3 = g["M"][:].rearrange("h (s u) -> h s u", u=8)
                Mprev_b = m0T[:].unsqueeze(2).to_broadcast([8, 16, 8])
                Mend = M3[:, :, 7]
            else:
                kb.scan(g["M"][:], cs["zeros8t"][:], g["a"][:], M_c[:, 0:1], ALU.add, ALU.max)
                M3 = g["M"][:].rearrange("h (s u) -> h s u", u=128)
                Mprev_b = M_c[:, 0:1].unsqueeze(2).to_broadcast([8, 1, 128])
                Mend = M3[:, :, 127]
            G43 = lambda q: G4[:, q, :].rearrange("h (s u) -> h s u", u=Lm)
            yield
            kb.cp(G4[:, 0, :], g["a"][:])
            yield
            kb.tt(G43(1), Mprev_b, M3, ALU.subtract)
            yield
            kb.actf(G4[:, 1, :], G4[:, 1, :], AF.Exp)
            yield
            kb.tt(G4[:, 2, :], g["Bn"][:], g["M"][:], ALU.subtract)
            yield
            kb.actf(G4[:, 2, :], G4[:, 2, :], AF.Exp)
            yield
            kb.tt(G43(3), a3 if sample else g["a"][:].rearrange("h (s u) -> h s u", u=128),
                  Mend.unsqueeze(2).to_broadcast([8, NCm, Lm]), ALU.subtract)
            yield
            kb.actf(G4[:, 3, :], G4[:, 3, :], AF.Exp)
            yield
            kb.ts(negM[:], g["M"][:], -1.0, None, ALU.mult)
            if sample:
                kb.tt(dec8[:, 0:16], m0T[:], Mend, ALU.subtract)
            else:
                kb.tt(dec8[:, 0:1], M_c[:, 0:1], Mend, ALU.subtract)
            yield
            kb.actf(dec8[:, 0:NCm], dec8[:, 0:NCm], AF.Exp)
            Bn3 = g["Bn"][:].rearrange("h (s u) -> h s u", u=Lm)
            yield
            kb.tt(mT8[:, 0:NCm], Mend, Bn3[:, :, Lm - 1], ALU.subtract)
            if sample:
                kb.dma(sm_o.rearrange("s h -> h s"), mT8[:, 0:16], allow_slow_non_contiguous=True)
            else:
                if n == last_prompt:
                    kb.dma(pm_o, mT8[:, 0:1])
                kb.cp(Bn_c[:], g["Bn"][:, 127:128], e=kb.pool)
                kb.cp(M_c[:], g["M"][:, 127:128], e=kb.pool)
            pb = pbank()
            for q in range(4):
                kb.mm(pb[:, q * 8:(q + 1) * 8], G4[:, q, :], cs["i8"][:])
            kb.cp(gtok[:].rearrange("p q h -> p (q h)"), pb[:, 0:32])
            kb.tt(ddec[:, 0:NCm, :], dec8[:, 0:NCm].unsqueeze(2).to_broadcast([8, NCm, 8]),
                  cs["i8"][:].unsqueeze(1).to_broadcast([8, NCm, 8]), ALU.mult)
            pb = pbank()
            kb.mm(pb[:, 0:NCm * 8], cs["ones8"][:], ddec[:, 0:NCm, :].rearrange("h c k -> h (c k)"))
            pv3 = pb[:, 0:NCm * 8].rearrange("p (c hp e) -> p c hp e", hp=4, e=2)
            kb.cp(decsel[0:64, 0:NCm, :], pv3[0:64, :, :, 0])
            kb.cp(decsel[64:128, 0:NCm, :], pv3[64:128, :, :, 1])
            kb.cp(vext[:, :, 0:64], TM[:, 1, :].rearrange("p (h i) -> p h i", i=64))
            mbias = cs["mbs"] if sample else cs["mbp"]
            def mlstm_hp(hp, S):
                M_ = S.ml
                ktok, kw, Wt, PT, hnd, st16 = M_["ktok"], M_["kw"], M_["Wt"], M_["PT"], M_["hnd"], M_["st16"]
                mk4, RqTm = S.rw["mk4"], S.rw["RqTm"]
                pb = pbank()
                kb.mm(pb[:, 0:128], QK[:, 4 + hp, :], identb[:])
                kb.tt(kw[:].rearrange("p (e d) -> p e d", e=2), pb[:, 0:128].rearrange("p (e d) -> p e d", e=2),
                      gtok[:, 3, 2 * hp:2 * hp + 2].unsqueeze(2).to_broadcast([128, 2, 64]), ALU.mult)
                pS, pE = ppair(), pbank()
                for e in range(2):
                    sl = slice(64 * e, 64 * e + 64)
                    kb.mm(pS[:, e, 0:128], QK[sl, 4 + hp, :], QK[sl, hp, :])
                for e in range(2):
                    kb.mm(pE[:, e * 128:(e + 1) * 128], cs["selh"][:, 2 * hp + e, :], negM[:], start=True, stop=False)
                    kb.mm(pE[:, e * 128:(e + 1) * 128], ident[:], mbias[:], start=False, stop=True)
                    kb.actf(Wt[:, e, :], pE[:, e * 128:(e + 1) * 128], AF.Exp, bias=gtok[:, 0, 2 * hp + e:2 * hp + e + 1])
                kb.tt(PT[:], pS[:, :, 0:128], Wt[:], ALU.mult)
                yield
                if sample:
                    for e in range(2):
                        kb.dma(Hs_in[64 * e:64 * e + 64, :, :], C0[:, 2 * hp + e, :, :].rearrange("s i j -> i s j"))
                    kb.dma(XY[0:16, 1024:1152], n0.rearrange("s h j -> s (h j)")[:, hp * 128:(hp + 1) * 128])
                    pb = pbank()
                    kb.mm(pb[:, 0:16], XY[0:16, 1024:1152], ident[0:16, 0:16])
                    kb.cp(H0s[:, :, 64], pb[:, 0:16])
                    per_seq_tiled(lambda s, sl: Hs_in[sl, s, :], lambda s, sl: ident[sl, sl],
                                  lambda sl, s0, v: kb.cp(H0s[sl, s0:s0 + 8, 0:64], v))
                if not sample:
                    pJp = ppair()
                    for e in range(2):
                        sl = slice(64 * e, 64 * e + 64)
                        kb.mm(pJp[:, e, 0:65], QK[sl, hp, :], Cpb[hp][sl, :])
                if sample:
                    for g0 in range(0, 16, G):
                        q3 = rq_group(QK[:, hp, :], g0, RqTm)
                        pq = ppair()
                        for e in range(2):
                            sl = slice(64 * e, 64 * e + 64)
                            for g_ in range(G):
                                kb.mm(pq[:, e, 0:65], q3[sl, g_, :], H0s[sl, g0 + g_, :], start=(g_ == 0), stop=(g_ == G - 1))
                        if g0 == 0:
                            kb.cp(hnd[:], pq[:, :, 0:65])
                        else:
                            kb.tt(hnd[:], hnd[:], pq[:, :, 0:65], ALU.add)
                    kb.tt(hnd[:], hnd[:], gtok[:, 1, 2 * hp:2 * hp + 2].unsqueeze(2).to_broadcast([128, 2, 65]), ALU.mult)
                else:
                    kb.tt(hnd[:], pJp[:, :, 0:65],
                          gtok[:, 1, 2 * hp:2 * hp + 2].unsqueeze(2).to_broadcast([128, 2, 65]), ALU.mult)
                yield
                pI = pbank()
                for e in range(2):
                    kb.mm(pI[:, e * 65:(e + 1) * 65], PT[:, e, :], vext[:, 2 * hp + e, :])
                kb.tt(hnd[:], hnd[:], pI[:, 0:130].rearrange("p (e c) -> p e c", e=2), ALU.add)
                kb.stt(st16[:, 0:2], hnd[:, :, 64], -1.0, hnd[:, :, 64], ALU.mult, ALU.max)
                kb.tt(st16[:, 0:2], st16[:, 0:2], gtok[:, 2, 2 * hp:2 * hp + 2], ALU.max)
                kb.recip(st16[:, 2:4], st16[:, 0:2])
                kb.tt(Hb[:, 2 * hp:2 * hp + 2, :], hnd[:, :, 0:64], st16[:, 2:4].unsqueeze(2).to_broadcast([128, 2, 64]), ALU.mult)
                yield
                if sample:
                    kb.tt(Hn[:], H0s[:], decsel[:, 0:16, hp:hp + 1].to_broadcast([128, 16, 65]), ALU.mult)
                    for g0 in range(0, 16, G):
                        pb = pbank()
                        for e in range(2):
                            sl = slice(64 * e, 64 * e + 64)
                            kb.tt(mk4[e][:, 0:G * 65].rearrange("p (c j) -> p c j", j=65),
                                  vext[:, 2 * hp + e, :].unsqueeze(1).to_broadcast([128, G, 65]),
                                  cs["rowms"][:, g0:g0 + G].unsqueeze(2).to_broadcast([128, G, 65]), ALU.mult,
                                  e=(kb.pool if e else None))
                            kb.mm(pb[sl, 0:G * 65], kw[:, sl], mk4[e][:, 0:G * 65])
                        kb.tt(Hn[:, g0:g0 + G, :], Hn[:, g0:g0 + G, :], pb[:, 0:G * 65].rearrange("p (s c) -> p s c", c=65), ALU.add)
                    per_seq_tiled(lambda s, sl: Hn[sl, s, 0:64], lambda s, sl: ident[sl, sl],
                                  lambda sl, s0, v: kb.cp(So[sl, s0:s0 + 8, :], v))
                    for e in range(2):
                        kb.dma(sC_o[:, 2 * hp + e, :, :].rearrange("s i j -> i s j"), So[64 * e:64 * e + 64, :, :])
                    kb.cp(XY[:, 1200:1216], Hn[:, :, 64])
                    pb = pbank()
                    kb.mm(pb[0:16, 0:128], XY[:, 1200:1216], ident[:])
                    kb.cp(XY[0:16, 1536:1664], pb[0:16, 0:128])
                    kb.dma(sn_o.rearrange("s h j -> s (h j)")[:, hp * 128:(hp + 1) * 128], XY[0:16, 1536:1664])
                else:
                    pb = pbank()
                    for e in range(2):
                        sl = slice(64 * e, 64 * e + 64)
                        kb.mm(pb[sl, 0:65], kw[:, sl], vext[:, 2 * hp + e, :])
                    kb.stt(Cp[hp][:, :], Cp[hp][:, :], decsel[:, 0, hp:hp + 1], pb[:, 0:65], ALU.mult, ALU.add)
                    kb.cp(Cpb[hp][:, :], Cp[hp][:, :], e=kb.pool)
                yield

            if sample:
                for hp_ in range(4):
                    yield from il([mlstm_hp(hp_, SETS[0])])
            else:
                yield from il([mlstm_hp(0, SETS[0]), mlstm_hp(1, SETS[1])])
                yield from il([mlstm_hp(2, SETS[0]), mlstm_hp(3, SETS[1])])
            Hbf = Hb[:].rearrange("p h i -> p (h i)")
            ct2 = ctmp[:, 4:8, :].rearrange("p a t -> p (a t)")
            gob = cacc[:, 0:4, :].rearrange("p a t -> p (a t)")
            gzb = cacc[:, 4:8, :].rearrange("p a t -> p (a t)")
            kb.actf(gob, TM[:, 2, :], AF.Sigmoid)
            kb.actf(gzb, TM[:, 3, :], AF.Sigmoid)
            kb.tt(gob, gob, gzb, ALU.mult, e=kb.pool)
            kb.tt(gob, gob, TM[:, 3, :], ALU.mult, e=kb.pool)
            kb.tt(gob, gob, gnw_b, ALU.mult, e=kb.pool)
            yield
            kb.tt(ct2.rearrange("p (h i) -> p h i", i=64), Hb[:], Hb[:], ALU.mult)
            yield
            kb.rsum(st8b[:, :, 1], ct2.rearrange("p (h i) -> p h i", i=64))
            yield
            rstd_from(st8b[:, :, 2], st8b[:, :, 1], 1.0 / 64, epsb[0:128, 0:1])
            yield
            kb.tt(Hb[:], Hb[:], st8b[:, :, 2:3].to_broadcast([128, 8, 64]), ALU.mult)
            yield
            kb.tt(OBb[:], Hbf, gob, ALU.mult)

        def g_final():
            if (not sample) and n == last_prompt:
                Sf = OA[:].rearrange("p (h j) -> p h j", j=64)
                heads_T(lambda hp_, sl: Hst[hp_][sl, :], Sf[:, 0:4, :])
                yield
                heads_T(lambda hp_, sl: Cp[hp_][sl, 0:64], Sf[:, 4:8, :])
                yield
                for e in range(2):
                    kb.dma(pS_o.rearrange("(hp e) i j -> e i hp j", e=2)[e], Sf[64 * e:64 * e + 64, 0:4, :])
                    kb.dma(pC_o.rearrange("(hp e) i j -> e i hp j", e=2)[e], Sf[64 * e:64 * e + 64, 4:8, :])
                    for hp_ in range(4):
                        kb.dma(pn_o[2 * hp_ + e:2 * hp_ + e + 1, :].rearrange("o j -> j o"), Cp[hp_][64 * e:64 * e + 64, 64:65],
                               allow_slow_non_contiguous=True)
            yield

        def g_tail_a():
            if sample:
                kb.memset(SETS[0].rw["RqTm"][:], 0.0, e=kb.pool)
            for half in range(2):
                pb = pbank()
                for k4 in range(4):
                    k = half * 4 + k4
                    kb.mm(pb[:, k4 * 128:(k4 + 1) * 128], (OAb if half == 0 else OBb)[:, k4 * 128:(k4 + 1) * 128], identb[:])
                kb.acp(oT[:, half * 4:half * 4 + 4, :].rearrange("p k t -> p (k t)"), pb[:, :])
                yield

        def g_tail_b():
            for gi in range(2):
                for half in range(2):
                    wp = epiece(gi * 2 + half).rearrange("p (k n) -> p k n", n=512)
                    pb = pbank()
                    for k in range(8):
                        kb.mm(pb[:, :], hT[:, k, :], wp[:, k, :], start=(k == 0), stop=(k == 7))
                    kb.actf((SGa if gi == 0 else xo)[:, half * 512:(half + 1) * 512], pb[:, :], AF.Sigmoid)
                    yield
            M1 = yt
            for gi in range(2):
                wp = epiece(4 + gi).rearrange("p (k n) -> p k n", n=1024)
                for half in range(2):
                    pb = pbank()
                    for kk_ in range(4):
                        kb.mm(pb[:, :], oT[:, gi * 4 + kk_, :], wp[:, kk_, half * 512:(half + 1) * 512],
                              start=(kk_ == 0), stop=(kk_ == 3))
                    hs = slice(half * 512, (half + 1) * 512)
                    if gi == 0:
                        kb.tt(M1[:, hs], SGa[:, hs], pb[:, :], ALU.mult)
                    else:
                        kb.tt(xo[:, hs], xo[:, hs], pb[:, :], ALU.mult)
                        kb.tt(MGb[:, hs], xo[:, hs], M1[:, hs], ALU.add)
                yield
            for half in range(2):
                pb = pbank()
                for k4 in range(4):
                    k = half * 4 + k4
                    kb.mm(pb[:, k4 * 128:(k4 + 1) * 128], MGb[:, k * 128:(k + 1) * 128], identb[:])
                kb.acp(mgT[:, half * 4:half * 4 + 4, :].rearrange("p k t -> p (k t)"), pb[:, :])
            gt_ = gate_s if sample else gate_p
            for half in range(2):
                wp = epiece(6 + half).rearrange("p (k n) -> p k n", n=512)
                pb = pbank()
                for k in range(8):
                    kb.mm(pb[:, :], mgT[:, k, :], wp[:, k, :], start=(k == 0), stop=(k == 7))
                hs = slice(half * 512, (half + 1) * 512)
                kb.tt(xo[:, hs], pb[:, :], gt_[:, hs], ALU.mult)
                yield
            kb.tt(xo[:], xo[:], x_t[:], ALU.add)
            kb.actf(yt[:], xo[:], AF.Square, accum=ssq[:, 2:3])
            rstd_from(ssq[:, 3:4], ssq[:, 2:3], 1.0 / D, epsb[:, 0:1])
            kb.stt(yt[:], xo[:], ssq[:, 3:4], gfin_b, ALU.mult, ALU.mult)
            kb.dma(y_o[n], yt[:])
            yield

        return dict(head=g_head(), rw=g_rwkv(), ml=g_mlstm(), tail_a=g_tail_a(), tail_b=g_tail_b(), final=g_final(), stage=stage)

    order = ([16] + list(range(16))) if order is None else order
    tiles = []
    if order:
        kb.dma(xt[0][:], xall[order[0]])
    for idx, n in enumerate(order):
        tiles.append(tile_body(n, n == 16, idx % 2))
    prev_tail = None
    for idx, n in enumerate(order):
        T_ = tiles[idx]

        def rwfull(T_=T_):
            yield from T_["head"]
            yield from T_["rw"]

        def tail_then_prefetch(tg=prev_tail, idx=idx):
            if tg is not None:
                yield from tg
            if idx + 1 < len(order):
                kb.dma(xt[(idx + 1) % 2][:], xall[order[idx + 1]])
            yield
        if n == 16:
            run_il([tail_then_prefetch()])
            run_il([rwfull()])
            run_il([T_["ml"]])
        else:
            run_il_w([(rwfull(), RW_W), (T_["ml"], 1), (tail_then_prefetch(), 1)])
        run_il([T_["tail_a"]])
        prev_tail = T_["tail_b"]
    if prev_tail is not None:
        run_il([prev_tail, tiles[-1]["final"]])
    kb.finish()
    _LAST_SEQ[0] = list(ring["seq"])
    return nc


_NC_CACHE = {}


def kernel(x_prompt, x_sample, c_prompt, c_sample, state_rwkv_shift, state_rwkv_S, state_mlstm_conv,
           state_mlstm_C, state_mlstm_n, state_mlstm_m, g_norm, w_ada, b_ada, w_in, mu_shift, w_decay2, w0,
           w_iclr2, a0, k_k, k_a, r_k, ln_w, ln_b, conv_w, conv_b, b_i, b_f, gn_w, w_up_a, w_up_b, w_out, g_final):
    f = lambda a: np.ascontiguousarray(np.asarray(a, dtype=np.float32))
    if "nc" not in _NC_CACHE:
        _NC_CACHE["nc"] = build_two_pass()
    nc = _NC_CACHE["nc"]
    fm = lambda v, nchunk: f(v).reshape(nchunk, 128).T
    b_ada_ = f(b_ada)[0]
    ka = f(k_a)[0]
    pfeat = np.zeros((128, 64), np.float32)
    pfeat[:, 0:8] = fm(g_norm[0], 8)
    pfeat[:, 8:16] = fm(b_ada_[0:1024], 8)
    pfeat[:, 16:24] = fm(b_ada_[1024:2048], 8)
    pfeat[:, 24:37] = fm(mu_shift[0], 13)
    pfeat[:, 37:41] = fm(w0[0], 4)
    pfeat[:, 41:45] = fm(a0[0], 4)
    pfeat[:, 45:49] = fm(k_k[0], 4)
    pfeat[:, 49:53] = fm(ka, 4)
    pfeat[:, 53:57] = fm(r_k[0], 4)
    pconv = np.zeros((128, 40), np.float32)
    cw = f(conv_w)[0]
    for j in range(4):
        pconv[:, j * 8:(j + 1) * 8] = fm(cw[j], 8)
    pconv[:, 32:40] = fm(conv_b[0], 8)
    prow = np.zeros((1, 6 * 1024), np.float32)
    prow[0, 0:1024] = b_ada_[2048:3072]
    prow[0, 1024:2048] = f(g_final)
    prow[0, 2048:2560] = f(ln_w)[0]
    prow[0, 2560:3072] = f(ln_b)[0]
    prow[0, 3072:3584] = f(gn_w)[0]
    pgate = np.zeros((8, 4), np.float32)
    pgate[:, 0] = f(b_i)[0]
    pgate[:, 1] = f(b_f)[0]
    wlora = np.concatenate([f(w_decay2)[0], f(w_iclr2)[0]], axis=0)
    w_in_ = f(w_in)[0]
    wua, wub, wo = f(w_up_a)[0], f(w_up_b)[0], f(w_out)[0]
    epw = np.zeros((8, 128, 4096), np.float32)
    kpn = lambda w_, nk: w_.reshape(nk, 128, -1).transpose(1, 0, 2).reshape(128, -1)
    for gi, c0 in enumerate((C_GLA, C_GLB)):
        for half in range(2):
            epw[gi * 2 + half] = kpn(w_in_[:, c0 + half * 512:c0 + (half + 1) * 512], 8)
    epw[4] = kpn(wua, 4)
    epw[5] = kpn(wub, 4)
    for half in range(2):
        epw[6 + half] = kpn(wo[:, half * 512:(half + 1) * 512], 8)
    xp, xs = f(x_prompt), f(x_sample)
    shared = {"w_ada": f(w_ada)[0], "w_in": w_in_, "epw": epw, "wlora": wlora, "pfeat": pfeat, "prow": prow,
              "pgate": pgate, "pconv": pconv}
    for k, v in _CONSTS.items():
        shared["c_" + k] = v
    in_maps = []
    for c in range(NCORES):
        sl = slice(16 * c, 16 * c + 16)
        m = dict(shared)
        m["xall"] = np.concatenate([xp[c].reshape(16, 128, D), xs[sl].reshape(1, 128, D)], axis=0)
        m["cvec"] = np.concatenate([f(c_prompt)[c:c + 1], f(c_sample)[sl]], axis=0)
        m["shift0"] = f(state_rwkv_shift)[0, sl]
        m["S0"] = f(state_rwkv_S)[0, sl]
        m["conv0"] = f(state_mlstm_conv)[0, sl].reshape(48, 1024)
        m["C0"] = f(state_mlstm_C)[0, sl]
        m["n0"] = f(state_mlstm_n)[0, sl]
        m["m0"] = f(state_mlstm_m)[0, sl]
        in_maps.append(m)
    res = run_bass_kernel_spmd(nc, in_maps, core_ids=list(range(NCORES)))
    R = res.results
    cat = lambda k: np.stack([np.asarray(R[c][k]) for c in range(NCORES)])
    y = cat("y")
    y_prompt = y[:, 0:16].reshape(8, 2048, D)
    y_sample = y[:, 16].reshape(128, 8, D)
    outs = (y_prompt, y_sample,
            cat("p_shift")[None], cat("p_S")[None], cat("p_conv")[None], cat("p_C")[None], cat("p_n")[None],
            cat("p_m").reshape(8, 8)[None],
            cat("s_shift").reshape(128, SHIFT_W)[None], cat("s_S").reshape(128, 8, 64, 64)[None],
            cat("s_conv").reshape(128, 3, 1024)[None], cat("s_C").reshape(128, 8, 64, 64)[None],
            cat("s_n").reshape(128, 8, 64)[None], cat("s_m").reshape(128, 8)[None])
    return tuple(np.ascontiguousarray(o, dtype=np.float32) for o in outs)
```
